# Optimizing a Trainium2 kernel written in Bass

```python
import jax, jax.numpy as jnp
from jax import lax
import numpy as np

D_MODEL = 1024
BATCH = 8
SEQ = 4096
DEPTH = 2

HG_HEADS = 4
HG_KDIM = 128
HG_VDIM = 128
HG_CHUNK = 64
ATT_GROUPS = ((128, 1), (512, 4), (2048, 16))
N_GROUPS = 3
ATT_HEADS = 4
ATT_DIM = 128
ROPE_THETA = 500000.0
ROPE_DIM = ATT_DIM // 4
D_FF = 4 * D_MODEL
EPS = 1e-6

HG_QK_W = HG_HEADS * HG_KDIM
HG_V_W = HG_HEADS * HG_VDIM
ATT_W = ATT_HEADS * ATT_DIM
ATT_QKV_W = 3 * N_GROUPS * ATT_W
SPLITS = (HG_QK_W, HG_QK_W, HG_V_W, HG_V_W, ATT_QKV_W, D_MODEL, D_MODEL)
N_IN = HG_QK_W + HG_QK_W + HG_V_W + HG_V_W + ATT_QKV_W + D_MODEL + D_MODEL

kernel_name = "hybrid_hgrn2_dilated_swa_gated_block"


def rms_norm(x, g):
    xf = x.astype(jnp.float32)
    y = xf * lax.rsqrt(jnp.mean(xf * xf, axis=-1, keepdims=True) + EPS)
    return (y * g.astype(jnp.float32)).astype(x.dtype)


def partial_rope(t, cos, sin):
    tf = t.astype(jnp.float32)
    c = cos[:, None, :]
    s = sin[:, None, :]
    half = ROPE_DIM // 2
    r1 = tf[..., :half]
    r2 = tf[..., half:ROPE_DIM]
    out = jnp.concatenate([r1 * c - r2 * s, r2 * c + r1 * s, tf[..., ROPE_DIM:]], axis=-1)
    return out.astype(t.dtype)


def hgrn2_chunked(q, log_f, k, v):
    B, S, H, K = q.shape
    V = v.shape[-1]
    C = HG_CHUNK
    nc = S // C

    def chunks(t):
        return t.reshape(B, nc, C, H, t.shape[-1]).swapaxes(0, 1)

    causal = jnp.tril(jnp.ones((C, C), dtype=bool))[None, :, :, None, None]

    def step(state, xs):
        qc, gc, kc, vc = xs
        b = jnp.cumsum(gc, axis=1)
        rel = jnp.where(causal, b[:, :, None] - b[:, None, :], -jnp.inf)
        attn = jnp.einsum('bthk,btshk,bshk->bhts', qc, jnp.exp(rel), kc)
        o = (jnp.einsum('bhts,bshv->bthv', attn, vc)
             + jnp.einsum('bthk,bhkv->bthv', qc * jnp.exp(b), state))
        b_last = b[:, -1]
        k_dec = kc * jnp.exp(b_last[:, None] - b)
        state = state * jnp.exp(b_last)[..., None] + jnp.einsum('bshk,bshv->bhkv', k_dec, vc)
        return state, o

    state0 = jnp.zeros((B, H, K, V), jnp.float32)
    _, o = lax.scan(step, state0, (chunks(q), chunks(log_f), chunks(k), chunks(v)))
    return o.swapaxes(0, 1).reshape(B, S, H, V)


def dilated_window_attention(q, k, v, window, dilation):
    B, S, H, Dh = q.shape
    L = window // dilation
    period = dilation * L
    Sp = -(-S // period) * period
    M = Sp // dilation
    nb = M // L
    pad = Sp - S

    def to_blocks(t):
        t = jnp.pad(t, ((0, 0), (0, pad), (0, 0), (0, 0)))
        t = t.reshape(B, M, dilation, H, Dh)
        t = t.transpose(0, 2, 3, 1, 4)
        return t.reshape(B, dilation, H, nb, L, Dh)

    def with_prev(t):
        prev = jnp.pad(t, ((0, 0), (0, 0), (0, 0), (1, 0), (0, 0), (0, 0)))[:, :, :, :-1]
        return jnp.concatenate([prev, t], axis=4)

    qb = to_blocks(q).astype(jnp.float32)
    kw = with_prev(to_blocks(k)).astype(jnp.float32)
    vw = with_prev(to_blocks(v)).astype(jnp.float32)

    scores = jnp.einsum('bzhnqd,bzhnkd->bzhnqk', qb, kw) * (Dh ** -0.5)
    i = jnp.arange(L)[:, None]
    j = jnp.arange(2 * L)[None, :]
    delta = L + i - j
    band = (delta >= 0) & (delta <= L)
    n = jnp.arange(nb)[:, None, None]
    mask = band[None] & ((n > 0) | (j[None] >= L))
    scores = jnp.where(mask, scores, -jnp.inf)
    m = jnp.max(scores, axis=-1, keepdims=True)
    p = jnp.exp(scores - m)
    den = jnp.sum(p, axis=-1, keepdims=True)
    o = jnp.einsum('bzhnqk,bzhnkd->bzhnqd', p, vw) / den
    lse = (m + jnp.log(den))[..., 0]

    o = o.reshape(B, dilation, H, M, Dh).transpose(0, 3, 1, 2, 4).reshape(B, Sp, H, Dh)[:, :S]
    lse = lse.reshape(B, dilation, H, M).transpose(0, 3, 1, 2).reshape(B, Sp, H)[:, :S]
    return o, lse


def setup_inputs(seed: int = 0) -> dict:
    key = jax.random.key(seed)
    ks = jax.random.split(key, 13)
    f32 = jnp.float32
    nrm = lambda k, shape, scale: jax.random.normal(k, shape, f32) * scale
    return {
        "x": nrm(ks[0], (BATCH, SEQ, D_MODEL), 1.0),
        "norm1_g": 1.0 + nrm(ks[1], (DEPTH, D_MODEL), 0.02),
        "w_in": nrm(ks[2], (DEPTH, D_MODEL, N_IN), D_MODEL ** -0.5),
        "hg_lower_bounds": nrm(ks[3], (DEPTH, HG_QK_W), 1.0),
        "hg_norm_g": 1.0 + nrm(ks[4], (DEPTH, HG_V_W), 0.02),
        "w_branch_a": nrm(ks[5], (DEPTH, HG_V_W, D_MODEL), HG_V_W ** -0.5),
        "w_branch_b": nrm(ks[6], (DEPTH, ATT_W, D_MODEL), ATT_W ** -0.5),
        "w_out": nrm(ks[7], (DEPTH, D_MODEL, D_MODEL), D_MODEL ** -0.5),
        "norm2_g": 1.0 + nrm(ks[8], (DEPTH, D_MODEL), 0.02),
        "w_up": nrm(ks[9], (DEPTH, D_MODEL, D_FF), D_MODEL ** -0.5),
        "w_down": nrm(ks[10], (DEPTH, D_FF, D_MODEL), D_FF ** -0.5),
        "final_norm_g": 1.0 + nrm(ks[11], (D_MODEL,), 0.02),
    }


def reference(x, norm1_g, w_in, hg_lower_bounds, hg_norm_g, w_branch_a, w_branch_b,
              w_out, norm2_g, w_up, w_down, final_norm_g):
    B, S, _ = x.shape
    f32 = jnp.float32
    split_idx = [int(c) for c in np.cumsum(SPLITS)[:-1]]

    pos = jnp.arange(S, dtype=f32)
    inv_freq = ROPE_THETA ** (-jnp.arange(0, ROPE_DIM, 2, dtype=f32) / ROPE_DIM)
    ang = pos[:, None] * inv_freq[None, :]
    cos, sin = jnp.cos(ang), jnp.sin(ang)

    lb_all = jnp.cumsum(jax.nn.softmax(hg_lower_bounds.astype(f32), axis=0), axis=0)
    lb_all = lb_all - lb_all[0:1]

    for l in range(DEPTH):
        h = rms_norm(x, norm1_g[l])
        z = h @ w_in[l]
        zq, zf, zi, zg, zatt, za, zb = jnp.split(z, split_idx, axis=-1)

        q_hg = jax.nn.silu(zq.astype(f32)).reshape(B, S, HG_HEADS, HG_KDIM)
        lb = lb_all[l]
        log_f = jnp.logaddexp(jnp.log(lb), jnp.log1p(-lb) + jax.nn.log_sigmoid(zf.astype(f32)))
        log_f = log_f.reshape(B, S, HG_HEADS, HG_KDIM)
        k_hg = -jnp.expm1(log_f)
        v_hg = zi.astype(f32).reshape(B, S, HG_HEADS, HG_VDIM)
        o_hg = hgrn2_chunked(q_hg, log_f, k_hg, v_hg)
        o_hg = rms_norm(o_hg, hg_norm_g[l].reshape(HG_HEADS, HG_VDIM))
        o_hg = o_hg * jax.nn.sigmoid(zg.astype(f32)).reshape(B, S, HG_HEADS, HG_VDIM)
        o_hg = o_hg.reshape(B, S, HG_V_W).astype(x.dtype)

        qkv = zatt.reshape(B, S, 3, N_GROUPS, ATT_HEADS, ATT_DIM)
        outs, lses = [], []
        for g, (window, dilation) in enumerate(ATT_GROUPS):
            qg = partial_rope(qkv[:, :, 0, g], cos, sin)
            kg = partial_rope(qkv[:, :, 1, g], cos, sin)
            vg = qkv[:, :, 2, g]
            o_g, lse_g = dilated_window_attention(qg, kg, vg, window, dilation)
            outs.append(o_g)
            lses.append(lse_g)
        w_grp = jax.nn.softmax(jnp.stack(lses, axis=0), axis=0)
        o_att = jnp.sum(w_grp[..., None] * jnp.stack(outs, axis=0), axis=0)
        o_att = o_att.reshape(B, S, ATT_W).astype(x.dtype)

        y = (jax.nn.sigmoid(za) * (o_hg @ w_branch_a[l])
             + jax.nn.sigmoid(zb) * (o_att @ w_branch_b[l]))
        x = x + y @ w_out[l]

        h2 = rms_norm(x, norm2_g[l])
        x = x + jnp.square(jax.nn.relu(h2 @ w_up[l])) @ w_down[l]

    return rms_norm(x, final_norm_g)
```

```python
from contextlib import ExitStack
import numpy as np
import concourse.bass as bass
import concourse.mybir as mybir
from concourse.bass_utils import run_bass_kernel_spmd

F32 = mybir.dt.float32
BF16 = mybir.dt.bfloat16
AF = mybir.ActivationFunctionType
ALU = mybir.AluOpType

S = 4096
D = 1024
NIN = 8704
DFF = 4096
EPS = 1e-6
DEPTH = 2
DIL = (1, 4, 16)
ATT_BASE = 2048
ZA_BASE = 2048 + 4608
ZB_BASE = ZA_BASE + 1024

ENGS = ("pe", "act", "dve", "pool", "sp")
STRICT_SAME = {"pe": False, "act": True, "dve": True, "pool": True, "sp": False}
DMA_RING = {"sp": 16}


class Op:
    __slots__ = ("eng", "fn", "deps", "is_dma", "slot", "need", "val", "tag")

    def __init__(self, eng, fn, deps, is_dma=False, slot=None):
        self.eng = eng
        self.fn = fn
        self.deps = deps
        self.is_dma = is_dma
        self.slot = slot
        self.need = False
        self.val = None
        self.tag = None


class Prog:
    def __init__(self):
        self.nc = bass.Bass("TRN2", target_bir_lowering=False)
        self.es = ExitStack()
        self.ops = []
        self.lastw = {}
        self.readers = {}
        self.ring_last = {q: [None] * n for q, n in DMA_RING.items()}
        self.ring_next = {q: 0 for q in DMA_RING}
        self.last_by_eng = {}
        self.dma_pending = []

    def sb(self, name, shape, dtype):
        return self.es.enter_context(self.nc.sbuf_tensor(name, list(shape), dtype))

    def ps(self, name, shape, dtype=F32):
        return self.es.enter_context(self.nc.psum_tensor(name, list(shape), dtype))

    def dram(self, name, shape, dtype, kind="Internal"):
        return self.nc.dram_tensor(name, list(shape), dtype, kind=kind)

    def _deps(self, reads, writes):
        deps = []
        for k in reads:
            w = self.lastw.get(k)
            if w is not None:
                deps.append(w)
        for k in writes:
            w = self.lastw.get(k)
            if w is not None:
                deps.append(w)
            deps.extend(self.readers.get(k, ()))
        return deps

    def _commit(self, op, reads, writes):
        for k in reads:
            lst = self.readers.setdefault(k, [])
            if not op.is_dma:
                lst[:] = [o for o in lst if o.is_dma or o.eng != op.eng]
            lst.append(op)
        for k in writes:
            self.lastw[k] = op
            self.readers[k] = []
        self.ops.append(op)
        if op.is_dma:
            self.dma_pending.append(op)
        else:
            self.last_by_eng[op.eng] = op

    def op(self, eng, fn, reads=(), writes=()):
        o = Op(eng, fn, self._deps(reads, writes))
        self._commit(o, reads, writes)
        return o

    def dma(self, out, in_, reads=(), writes=(), q="sp"):
        slot = self.ring_next[q]
        self.ring_next[q] = (slot + 1) % DMA_RING[q]
        deps = self._deps(reads, writes)
        prev = self.ring_last[q][slot]
        if prev is not None:
            deps.append(prev)
        o = Op(q, lambda e, out=out, in_=in_: e.dma_start(out=out, in_=in_).annotate(f'DMA {out.name}{list(out.shape)}@{out.offset} <- {in_.name}{list(in_.shape)}@{in_.offset}'), deps, True, slot)
        self.ring_last[q][slot] = o
        o.tag = f'{out.name}@{out.offset}<-{in_.name}@{in_.offset}'
        self._commit(o, reads, writes)
        return o

    def barrier(self):
        pend = list(self.last_by_eng.values()) + list(self.dma_pending)
        self.dma_pending = []
        for e in ENGS:
            deps = [o for o in pend if o.is_dma or o.eng != e or STRICT_SAME[e]]
            self.ops.append(Op(e, None, deps))
        self.lastw = {}
        self.readers = {}

    def finalize(self, final_waits=()):
        nc = self.nc
        for o in self.ops:
            for d in o.deps:
                if d.is_dma or d.eng != o.eng or STRICT_SAME[o.eng]:
                    d.need = True
        for o in final_waits:
            o.need = True
        csem = {e: self.es.enter_context(nc.semaphore(f"c_{e}")) for e in ENGS if e != "sp"}
        dsem = {q: [self.es.enter_context(nc.semaphore(f"d_{q}{i}")) for i in range(n)]
                for q, n in DMA_RING.items()}
        ccount = {e: 0 for e in ENGS}
        dcount = {q: [0] * n for q, n in DMA_RING.items()}
        waited = {e: {} for e in ENGS}
        streams = {e: [] for e in ENGS}
        for o in self.ops:
            st = streams[o.eng]
            w = waited[o.eng]
            need = {}
            for d in o.deps:
                if not d.is_dma and d.eng == o.eng and not STRICT_SAME[o.eng]:
                    continue
                key = ("d", d.eng, d.slot) if d.is_dma else ("c", d.eng)
                if w.get(key, 0) < d.val and need.get(key, 0) < d.val:
                    need[key] = d.val
            for key, v in need.items():
                sem = dsem[key[1]][key[2]] if key[0] == "d" else csem[key[1]]
                st.append(("w", sem, v))
                w[key] = v
            if o.fn is None:
                continue
            if o.is_dma:
                dcount[o.eng][o.slot] += 16
                o.val = dcount[o.eng][o.slot]
                st.append(("i", o.fn, dsem[o.eng][o.slot], 16, o.tag))
            elif o.need:
                ccount[o.eng] += 1
                o.val = ccount[o.eng]
                st.append(("i", o.fn, csem[o.eng], 1))
            else:
                o.val = ccount[o.eng] + 1
                st.append(("i", o.fn, None, 0))
        for o in final_waits:
            key = ("d", o.eng, o.slot) if o.is_dma else ("c", o.eng)
            sem = dsem[key[1]][key[2]] if key[0] == "d" else csem[key[1]]
            streams["sp"].append(("w", sem, o.val))
        self.streams = streams
        self.stats = dict(n_ops=len(self.ops), per_eng={e: len(s) for e, s in streams.items()}, ccount=ccount)

        def run(eng_obj, items):
            for it in items:
                if it[0] == "w":
                    eng_obj.wait_ge(it[1], it[2])
                else:
                    ins = it[1](eng_obj)
                    if it[2] is not None:
                        ins.then_inc(it[2], it[3])

        with nc.Block() as block:
            @block.tensor
            def _(e):
                run(e, streams["pe"])

            @block.scalar
            def _(e):
                run(e, streams["act"])

            @block.vector
            def _(e):
                run(e, streams["dve"])

            @block.gpsimd
            def _(e):
                run(e, streams["pool"])

            @block.sync
            def _(e):
                run(e, streams["sp"])
        self.es.close()
        return nc


class Arena:
    def __init__(self, tile, n):
        self.t = tile
        self.n = n
        self.off = 0

    def reset(self):
        self.off = 0

    def alloc(self, n):
        assert self.off + n <= self.n, (self.off, n, self.n)
        ap = self.t[:, self.off:self.off + n]
        self.off += n
        return ap


def CALL(name, *a, **k):
    return lambda e: getattr(e, name)(*a, **k)


def v3(ap, c):
    return ap.rearrange("p (c t) -> p c t", c=c)


def build(layers, final, dbg=(), maxphase=5):
    P = Prog()
    nl = DEPTH
    x_in = P.dram("x", [S, D], F32, "ExternalInput").ap()
    w_in = P.dram("w_in", [nl, D, NIN], F32, "ExternalInput").ap()
    w_a = P.dram("w_a", [nl, 512, D], F32, "ExternalInput").ap()
    w_b = P.dram("w_b", [nl, 512, D], F32, "ExternalInput").ap()
    w_out = P.dram("w_out", [nl, D, D], F32, "ExternalInput").ap()
    w_up = P.dram("w_up", [nl, D, DFF], F32, "ExternalInput").ap()
    w_dn = P.dram("w_dn", [nl, DFF, D], F32, "ExternalInput").ap()
    g1T_d = P.dram("g1T", [nl, 128, 8], F32, "ExternalInput").ap()
    g2T_d = P.dram("g2T", [nl, 128, 8], F32, "ExternalInput").ap()
    gnT_d = P.dram("gnT", [nl, 128, 4], F32, "ExternalInput").ap()
    hlb_d = P.dram("hlb", [nl, 128, 512], F32, "ExternalInput").ap()
    gf_d = P.dram("gf", [128, D], F32, "ExternalInput").ap()
    c_ident = P.dram("c_ident", [128, 128], F32, "ExternalInput").ap()
    c_tri = P.dram("c_tri", [128, 128], F32, "ExternalInput").ap()
    c_trev = P.dram("c_trev", [128, 128], F32, "ExternalInput").ap()
    c_band = P.dram("c_band", [128, 256], F32, "ExternalInput").ap()
    c_negband = P.dram("c_negband", [128, 256], F32, "ExternalInput").ap()
    c_pm = P.dram("c_pm", [32, 32], F32, "ExternalInput").ap()
    c_rm = P.dram("c_rm", [128, 2], F32, "ExternalInput").ap()
    c_cos = P.dram("c_cos", [32, S], F32, "ExternalInput").ap()
    c_sin = P.dram("c_sin", [32, S], F32, "ExternalInput").ap()
    out_d = P.dram("out", [S, D], F32, "ExternalOutput").ap()
    xs = P.dram("xs", [S, D], F32).ap()
    ohg_d = P.dram("ohg_d", [512, S], BF16).ap()
    oatt_d = P.dram("oatt_d", [512, S], BF16).ap()
    dbg_d = {k: P.dram("dbg_" + k, shp, F32, "ExternalOutput").ap() for k, shp in dbg}

    A_t = P.sb("arenaA", [128, 32768], BF16)
    B_t = P.sb("arenaB", [128, 46080], BF16)
    C_t = P.sb("arenaC", [128, 10752], F32)
    Bn = Arena(B_t, 46080)
    Cn = Arena(C_t, 10752)
    hT = v3(A_t[:, :], 8)
    ident = P.sb("ident", [128, 128], BF16)
    onesb = P.sb("onesb", [128, 128], BF16)
    trif = P.sb("trif", [128, 128], F32)
    trevf = P.sb("trevf", [128, 128], F32)
    band = P.sb("band", [128, 256], BF16)
    negband = P.sb("negband", [128, 256], BF16)
    trib = P.sb("trib", [128, 128], BF16)
    trevb = P.sb("trevb", [128, 128], BF16)
    pm = P.sb("pm", [32, 32], BF16)
    rm = P.sb("rm", [128, 2], F32)
    epsb = P.sb("epsb", [128, 1], F32)
    cf = P.sb("cf", [128, 256], F32)
    gsm = P.sb("gsm", [128, 24], F32)
    ssb = P.sb("ssb", [128, 4], F32)
    dec = P.sb("dec", [128, 16], F32)
    tp = P.ps("tp", [128, 1024], BF16)
    bk = [P.ps(f"bk{i}", [128, 512], F32) for i in range(7)]

    fin = []

    def ldconst(dst, src, rows=128, cols=128, cast=True):
        P.dma(cf[0:rows, 0:cols], src, writes=["cf"])
        P.op("dve", CALL("tensor_copy", out=dst, in_=cf[0:rows, 0:cols]), reads=["cf"], writes=["const"])

    ldconst(ident[:], c_ident)
    ldconst(band[:], c_band, 128, 256)
    ldconst(negband[:], c_negband, 128, 256)
    ldconst(trib[:], c_tri)
    ldconst(trevb[:], c_trev)
    ldconst(pm[:], c_pm, 32, 32)
    P.dma(trif[:], c_tri, writes=["const"])
    P.dma(trevf[:], c_trev, writes=["const"])
    P.dma(rm[:], c_rm, writes=["const"])
    P.op("dve", CALL("memset", onesb[:], 1.0), writes=["const"])
    P.op("dve", CALL("memset", epsb[:], EPS), writes=["const"])
    P.barrier()

    stg_i = [0]

    cast_i = [0]

    def load_w(dst3, src3, kc, n, stg, wkey="w", cast_engs=("dve", "act"), extra_w=()):
        cap = stg_cap[0]
        ncol = max(1, min(n, cap // kc))
        for c0 in range(0, n, ncol):
            c1 = min(n, c0 + ncol)
            i = stg_i[0] % len(stg)
            stg_i[0] += 1
            sv = v3(stg[i][:, 0:kc * (c1 - c0)], kc)
            P.dma(sv, src3[:, :, c0:c1], writes=[("stg", i)])
            ce = cast_engs[cast_i[0] % len(cast_engs)]
            cast_i[0] += 1
            if ce == "act":
                P.op("act", CALL("activation", out=dst3[:, :, c0:c1], in_=sv, func=AF.Copy), reads=[("stg", i)], writes=[wkey] + list(extra_w))
            else:
                P.op("dve", CALL("tensor_copy", out=dst3[:, :, c0:c1], in_=sv), reads=[("stg", i)], writes=[wkey] + list(extra_w))

    stg_cap = [2048]

    def norm_tile(xt, xkey, dst3, gbc, hs, par):
        ss = ssb[:, 2 * par:2 * par + 1]
        rs = ssb[:, 2 * par + 1:2 * par + 2]
        hk = ("hs", par)
        P.op("act", CALL("activation", out=hs, in_=xt, func=AF.Square, accum_out=ss),
             reads=[xkey], writes=[hk, ("ss", par)])
        P.op("act", CALL("activation", out=rs, in_=ss, func=AF.Ln, scale=1.0 / D, bias=epsb[:, 0:1]),
             reads=[("ss", par)], writes=[("rs", par)])
        P.op("act", CALL("activation", out=rs, in_=rs, func=AF.Exp, scale=-0.5),
             reads=[("rs", par)], writes=[("rs", par)])
        P.op("act", CALL("activation", out=hs, in_=xt, func=AF.Copy, scale=rs),
             reads=[xkey, ("rs", par)], writes=[hk])
        for c in range(8):
            P.op("pe", CALL("transpose", out=tp[:, c * 128:(c + 1) * 128], in_=hs[:, c * 128:(c + 1) * 128],
                                                  identity=ident[:]), reads=[hk], writes=["tp"])
        P.op("dve", CALL("tensor_tensor", out=dst3, in0=v3(tp[:, :], 8), in1=v3(gbc, 8), op=ALU.mult),
             reads=["tp", "gbc"], writes=["hTw"])
        return rs

    def make_gbc(gbc, col0):
        for c in range(8):
            P.op("dve", CALL("tensor_scalar", out=gbc[:, c * 128:(c + 1) * 128], in0=onesb[:],
                                                       scalar1=gsm[:, col0 + c:col0 + c + 1], scalar2=None, op0=ALU.mult),
                 reads=["gsm"], writes=["gbc"])


    def dump_bf(src_, row0):
        Bn.reset(); Cn.reset()
        dtile = Cn.alloc(4096)
        dbf = Bn.alloc(4096)
        for c in range(4):
            P.dma(dbf, src_[c * 128:(c + 1) * 128, :], writes=["dbf"])
            P.op("dve", CALL("tensor_copy", out=dtile, in_=dbf), reads=["dbf"], writes=["dt"])
            fin.append(P.dma(dbg_d["mix"][row0 + c * 128:row0 + (c + 1) * 128, :], dtile, reads=["dt"]))
        P.barrier()

    for li, l in enumerate(layers):
        xsrc = x_in if li == 0 else xs
        last = final and (li == len(layers) - 1)
        P.dma(gsm[:, 0:8], g1T_d[l], writes=["gsm"])
        P.dma(gsm[:, 8:16], g2T_d[l], writes=["gsm"])
        P.dma(gsm[:, 16:20], gnT_d[l], writes=["gsm"])

        Bn.reset(); Cn.reset()
        gbc = Bn.alloc(1024)
        hsb = [Bn.alloc(1024) for _ in range(2)]
        xtb = [Cn.alloc(1024) for _ in range(2)]
        make_gbc(gbc, 0)
        for i in range(32):
            p = i % 2
            P.dma(xtb[p], xsrc[i * 128:(i + 1) * 128, :], writes=[("xt", p)])
            norm_tile(xtb[p], ("xt", p), hT[:, :, i * 128:(i + 1) * 128], gbc, hsb[p], p)
        P.barrier()
        if "hT" in dbg_d and li == 0:
            Cn.reset()
            dtile = Cn.alloc(4096)
            for c in range(8):
                P.op("dve", CALL("tensor_copy", out=dtile, in_=hT[:, c, :]), writes=["dt"])
                fin.append(P.dma(dbg_d["hT"][c * 128:(c + 1) * 128, :], dtile, reads=["dt"]))
            P.barrier()

        if maxphase < 2:
            continue
        Bn.reset(); Cn.reset()
        Wq = v3(Bn.alloc(4096), 8); Wf = v3(Bn.alloc(4096), 8); Wi = v3(Bn.alloc(4096), 8); Wg = v3(Bn.alloc(4096), 8)
        qT = [v3(Bn.alloc(2048), 4) for _ in range(2)]
        sgT = [v3(Bn.alloc(2048), 4) for _ in range(2)]
        sq = Bn.alloc(512)
        kf = [Bn.alloc(512) for _ in range(2)]
        Vt = [Bn.alloc(512) for _ in range(3)]
        kdec = [[[Bn.alloc(128) for _ in range(2)] for _ in range(4)] for _ in range(2)]
        ktT = [Bn.alloc(128) for _ in range(2)]
        qtT = [[Bn.alloc(128) for _ in range(4)] for _ in range(2)]
        ATs = [[Bn.alloc(128) for _ in range(4)] for _ in range(2)]
        Sbf_flat = Bn.alloc(512)
        Sbf = v3(Sbf_flat, 4)
        SbfA = v3(Bn.alloc(512), 4)
        o2 = Bn.alloc(512)
        Ghi = [Bn.alloc(512) for _ in range(2)]
        Glo = [Bn.alloc(512) for _ in range(2)]
        outt = [Bn.alloc(512) for _ in range(2)]
        stg = [Cn.alloc(2048) for _ in range(2)]
        stg_cap[0] = 2048
        Fm = Cn.alloc(512)
        Gt = [Cn.alloc(512) for _ in range(2)]
        eBT = [Cn.alloc(128) for _ in range(2)]
        enBT = [Cn.alloc(128) for _ in range(2)]
        eR = [Cn.alloc(128) for _ in range(2)]
        Sst_flat = Cn.alloc(512)
        Sst = v3(Sst_flat, 4)
        oTs = v3(Cn.alloc(2048), 4)
        rsT = Cn.alloc(512)
        if l > 0:
            lbt = Cn.alloc(512); omlb = Cn.alloc(512)
            P.dma(lbt, hlb_d[l], writes=["lbt"])
            P.dma(omlb, hlb_d[0], writes=["omlb"])
            P.op("dve", CALL("tensor_tensor", out=lbt, in0=lbt, in1=omlb, op=ALU.subtract), reads=["lbt", "omlb"], writes=["lbt"])
            P.op("act", CALL("activation", out=lbt, in_=lbt, func=AF.Sigmoid), reads=["lbt"], writes=["lbt"])
            P.op("dve", CALL("tensor_scalar", out=omlb, in0=lbt, scalar1=-1.0, scalar2=1.0, op0=ALU.mult, op1=ALU.add),
                 reads=["lbt"], writes=["omlb"])
        wsrc = w_in[l].rearrange("(kc p) n -> p kc n", p=128)
        for Wt, c0 in ((Wq, 0), (Wf, 512), (Wi, 1024), (Wg, 1536)):
            load_w(Wt, wsrc[:, :, c0:c0 + 512], 8, 512, stg)
        P.op("dve", CALL("memset", Sst_flat, 0.0), writes=[("S", h_) for h_ in range(4)])
        P.op("dve", CALL("memset", Sbf_flat, 0.0), writes=[("Sb", h_) for h_ in range(4)])
        XB = bk[3]; YB = bk[6]; FI = bk[2]
        npb = [0]

        def proj_qg(J):
            jp = J % 2
            tokJ = slice(J * 512, (J + 1) * 512)
            for h in range(4):
                hs_ = slice(h * 128, (h + 1) * 128)
                for kind, Wt in (("q", Wq), ("g", Wg)):
                    pb = bk[npb[0] % 2]; pk = ("pb", npb[0] % 2); npb[0] += 1
                    for kc in range(8):
                        P.op("pe", CALL("matmul", pb[:, :], lhsT=Wt[:, kc, hs_], rhs=hT[:, kc, tokJ], start=(kc == 0), stop=(kc == 7)),
                             reads=["w"], writes=[pk])
                    if kind == "g":
                        P.op("act", CALL("activation", out=sgT[jp][:, h, :], in_=pb[:, :], func=AF.Sigmoid), reads=[pk], writes=[("sgT", jp, h)])
                    else:
                        P.op("act", CALL("activation", out=sq, in_=pb[:, :], func=AF.Sigmoid), reads=[pk], writes=["sq"])
                        P.op("dve", CALL("tensor_tensor", out=qT[jp][:, h, :], in0=pb[:, :], in1=sq, op=ALU.mult),
                             reads=[pk, "sq"], writes=[("qT", jp, h)])

        def proj_fi(i):
            tb = i % 2; tv = i % 3
            tok = slice(i * 128, (i + 1) * 128)
            for kc in range(8):
                P.op("pe", CALL("matmul", bk[0][:, :], lhsT=hT[:, kc, tok], rhs=Wf[:, kc, :], start=(kc == 0), stop=(kc == 7)),
                     reads=["w"], writes=[("pb", 0)])
            for kc in range(8):
                P.op("pe", CALL("matmul", bk[1][:, :], lhsT=hT[:, kc, tok], rhs=Wi[:, kc, :], start=(kc == 0), stop=(kc == 7)),
                     reads=["w"], writes=[("pb", 1)])
            P.op("act", CALL("activation", out=Fm, in_=bk[0][:, :], func=AF.Sigmoid), reads=[("pb", 0)], writes=["Fm"])
            P.op("act", CALL("activation", out=Vt[tv], in_=bk[1][:, :], func=AF.Copy), reads=[("pb", 1)], writes=[("V", tv)])
            if l > 0:
                P.op("dve", CALL("tensor_tensor", out=Fm, in0=Fm, in1=omlb, op=ALU.mult), reads=["Fm", "omlb"], writes=["Fm"])
                P.op("dve", CALL("tensor_tensor", out=Fm, in0=Fm, in1=lbt, op=ALU.add), reads=["Fm", "lbt"], writes=["Fm"])
            P.op("act", CALL("activation", out=Gt[tb], in_=Fm, func=AF.Ln), reads=["Fm"], writes=[("Gf", tb)])
            P.op("dve", CALL("tensor_scalar", out=kf[tb], in0=Fm, scalar1=-1.0, scalar2=1.0, op0=ALU.mult, op1=ALU.add),
                 reads=["Fm"], writes=[("kf", tb)])
            P.op("pool", CALL("tensor_copy", out=Ghi[tb], in_=Gt[tb]), reads=[("Gf", tb)], writes=[("Ghi", tb)])
            P.op("dve", CALL("tensor_tensor", out=Glo[tb], in0=Gt[tb], in1=Ghi[tb], op=ALU.subtract),
                 reads=[("Gf", tb), ("Ghi", tb)], writes=[("G", tb)])

        def AB_A_pe(i, pair):
            tb = i % 2
            for h in pair:
                hb = h % 2; hs_ = slice(h * 128, (h + 1) * 128); bS = bk[4 + hb]; kS = ("bS", hb)
                P.op("pe", CALL("matmul", bS[:, 0:128], lhsT=Ghi[tb][:, hs_], rhs=trib[:], start=True, stop=False),
                     reads=[("G", tb), ("Ghi", tb)], writes=[kS])
                P.op("pe", CALL("matmul", bS[:, 0:128], lhsT=Glo[tb][:, hs_], rhs=trib[:], start=False, stop=True),
                     reads=[("G", tb), ("Ghi", tb)], writes=[kS])
                P.op("pe", CALL("matmul", bS[:, 128:256], lhsT=trevb[:], rhs=Ghi[tb][:, hs_], start=True, stop=False),
                     reads=[("G", tb), ("Ghi", tb)], writes=[kS])
                P.op("pe", CALL("matmul", bS[:, 128:256], lhsT=trevb[:], rhs=Glo[tb][:, hs_], start=False, stop=True),
                     reads=[("G", tb), ("Ghi", tb)], writes=[kS])
                P.op("pe", CALL("transpose", out=tp[:, hb * 128:(hb + 1) * 128], in_=kf[tb][:, hs_], identity=ident[:]),
                     reads=[("kf", tb)], writes=["tp"])

        def AB_A_act(i, pair):
            tb = i % 2
            for h in pair:
                hb = h % 2; bS = bk[4 + hb]; kS = ("bS", hb)
                P.op("act", CALL("activation", out=eBT[hb], in_=bS[:, 0:128], func=AF.Exp), reads=[kS], writes=[("eBT", hb)])
                P.op("act", CALL("activation", out=enBT[hb], in_=bS[:, 0:128], func=AF.Exp, scale=-1.0), reads=[kS], writes=[("enBT", hb)])
                P.op("act", CALL("activation", out=eR[hb], in_=bS[:, 128:256], func=AF.Exp), reads=[kS], writes=[("eR", hb)])
                di = (tb * 4 + h) * 2
                P.op("act", CALL("activation", out=dec[:, di:di + 2], in_=bS[:, 63:128:64], func=AF.Exp), reads=[kS], writes=[("dec", tb, h)])

        def AB_A_dve(i, pair):
            J, t = divmod(i, 4)
            tb = i % 2; jp = J % 2
            for h in pair:
                hb = h % 2; hs_ = slice(h * 128, (h + 1) * 128)
                for c in range(2):
                    P.op("dve", CALL("scalar_tensor_tensor", out=kdec[tb][h][c], in0=kf[tb][:, hs_], scalar=rm[:, c:c + 1], in1=eR[hb],
                                     op0=ALU.mult, op1=ALU.mult), reads=[("kf", tb), ("eR", hb)], writes=[("kdec", tb, h)])
                P.op("dve", CALL("tensor_tensor", out=ktT[hb], in0=tp[:, hb * 128:(hb + 1) * 128], in1=enBT[hb], op=ALU.mult),
                     reads=["tp", ("enBT", hb)], writes=[("ktT", hb)])
                P.op("dve", CALL("tensor_tensor", out=qtT[tb][h], in0=qT[jp][:, h, t * 128:(t + 1) * 128], in1=eBT[hb], op=ALU.mult),
                     reads=[("qT", jp, h), ("eBT", hb)], writes=[("qtT", tb, h)])

        def AB_B_pe(i, pair):
            tb = i % 2
            for h in pair:
                hb = h % 2; bS = bk[4 + hb]; kS = ("bS", hb)
                P.op("pe", CALL("matmul", bS[:, 256:384], lhsT=ktT[hb], rhs=qtT[tb][h], start=True, stop=True),
                     reads=[("ktT", hb), ("qtT", tb, h)], writes=[kS])

        def AB_B_dve(i, pair):
            tb = i % 2
            for h in pair:
                hb = h % 2; bS = bk[4 + hb]; kS = ("bS", hb)
                P.op("dve", CALL("tensor_tensor", out=ATs[tb][h], in0=bS[:, 256:384], in1=trif[:], op=ALU.mult),
                     reads=[kS], writes=[("ATs", tb, h)])

        def AB(i, pair):
            AB_A_pe(i, pair); AB_A_act(i, pair); AB_A_dve(i, pair); AB_B_pe(i, pair); AB_B_dve(i, pair)

        def xb(h, c):
            bank, key = ((bk[3], "XB0"), (FI, "pfi"))[h // 2]
            off = ((h % 2) * 2 + c) * 128
            return bank[:, off:off + 128], key

        def CF_C(i):
            tb = i % 2; tv = i % 3
            for h in range(4):
                hs_ = slice(h * 128, (h + 1) * 128)
                for c in range(2):
                    xa, xk = xb(h, c)
                    P.op("pe", CALL("matmul", xa, lhsT=kdec[tb][h][c], rhs=Vt[tv][:, hs_], start=True, stop=True),
                         reads=[("kdec", tb, h), ("V", tv)], writes=[xk])
                P.op("pe", CALL("matmul", YB[:, h * 128:h * 128 + 64], lhsT=Vt[tv][:, hs_], rhs=ATs[tb][h][:, 0:64], start=True, stop=False),
                     reads=[("V", tv), ("ATs", tb, h)], writes=["YB"])
                P.op("pe", CALL("matmul", YB[:, h * 128:h * 128 + 64], lhsT=Sbf[:, h, :], rhs=qtT[tb][h][:, 0:64], start=False, stop=True),
                     reads=[("Sb", h), ("qtT", tb, h)], writes=["YB"])

        def CF_D(i, c):
            tb = i % 2
            for h in range(4):
                di = (tb * 4 + h) * 2 + c
                xa, xk = xb(h, c)
                P.op("dve", CALL("scalar_tensor_tensor", out=Sst[:, h, :], in0=Sst[:, h, :], scalar=dec[:, di:di + 1], in1=xa,
                                 op0=ALU.mult, op1=ALU.add), reads=[xk, ("dec", tb, h), ("S", h)], writes=[("S", h)])
            for h in range(4):
                if c == 0:
                    P.op("pool", CALL("tensor_copy", out=SbfA[:, h, :], in_=Sst[:, h, :]), reads=[("S", h)], writes=[("SbA", h)])
                else:
                    P.op("pool", CALL("tensor_copy", out=Sbf[:, h, :], in_=Sst[:, h, :]), reads=[("S", h)], writes=[("Sb", h)])

        def CF_E(i):
            tb = i % 2; tv = i % 3
            for h in range(4):
                hs_ = slice(h * 128, (h + 1) * 128)
                P.op("pe", CALL("matmul", YB[:, h * 128 + 64:h * 128 + 128], lhsT=Vt[tv][:, hs_], rhs=ATs[tb][h][:, 64:128], start=True, stop=False),
                     reads=[("V", tv), ("ATs", tb, h)], writes=["YB"])
                P.op("pe", CALL("matmul", YB[:, h * 128 + 64:h * 128 + 128], lhsT=SbfA[:, h, :], rhs=qtT[tb][h][:, 64:128], start=False, stop=True),
                     reads=[("SbA", h), ("qtT", tb, h)], writes=["YB"])

        def CF_evac(i):
            J, t = divmod(i, 4)
            P.op("act", CALL("activation", out=oTs[:, :, t * 128:(t + 1) * 128], in_=v3(YB[:, :], 4), func=AF.Copy),
                 reads=["YB"], writes=[("oTs", h_) for h_ in range(4)])

        def fin_J(J):
            jp = J % 2
            tokJ = slice(J * 512, (J + 1) * 512)
            for h in range(4):
                P.op("act", CALL("activation", out=o2, in_=oTs[:, h, :], func=AF.Square), reads=[("oTs", h)], writes=["o2"])
                P.op("pe", CALL("matmul", bk[0][:, :], lhsT=onesb[:], rhs=o2, start=True, stop=True), reads=["o2"], writes=[("pb", 0)])
                P.op("act", CALL("activation", out=rsT, in_=bk[0][:, :], func=AF.Ln, scale=1.0 / 128, bias=epsb[:, 0:1]),
                     reads=[("pb", 0)], writes=["rsT"])
                P.op("act", CALL("activation", out=rsT, in_=rsT, func=AF.Exp, scale=-0.5), reads=["rsT"], writes=["rsT"])
                P.op("dve", CALL("scalar_tensor_tensor", out=rsT, in0=oTs[:, h, :], scalar=gsm[:, 16 + h:17 + h], in1=rsT,
                                 op0=ALU.mult, op1=ALU.mult), reads=[("oTs", h), "rsT", "gsm"], writes=["rsT"])
                ob = h % 2
                P.op("dve", CALL("tensor_tensor", out=outt[ob], in0=rsT, in1=sgT[jp][:, h, :], op=ALU.mult),
                     reads=["rsT", ("sgT", jp, h)], writes=[("outt", ob)])
                P.dma(ohg_d[h * 128:(h + 1) * 128, tokJ], outt[ob], reads=[("outt", ob)], writes=["ohg_d"])

        proj_qg(0)
        proj_fi(0)
        proj_fi(1)
        AB(0, (0, 1)); AB(0, (2, 3))
        for i in range(32):
            J, t = divmod(i, 4)
            nxt = i + 1 < 32
            if nxt and (i + 1) % 4 == 0:
                proj_qg(J + 1)
            if i + 2 < 32:
                proj_fi(i + 2)
            CF_C(i)
            if nxt:
                AB_A_pe(i + 1, (0, 1))
            CF_D(i, 0)
            CF_D(i, 1)
            CF_E(i)
            if nxt:
                AB_A_act(i + 1, (0, 1)); AB_A_dve(i + 1, (0, 1)); AB_B_pe(i + 1, (0, 1)); AB_B_dve(i + 1, (0, 1))
            CF_evac(i)
            if nxt:
                AB(i + 1, (2, 3))
            if t == 3:
                fin_J(J)
        P.barrier()
        if "mix" in dbg_d and li == 0:
            dump_bf(ohg_d, 0)
        if maxphase < 3:
            continue
        Bn.reset(); Cn.reset()
        Wqkv2 = [[v3(Bn.alloc(1024), 8) for _ in range(3)] for _ in range(2)]
        qkT = [Bn.alloc(4096) for _ in range(2)]
        Vb = v3(Bn.alloc(4096), 32)
        pT = [Bn.alloc(256) for _ in range(3)]
        outt = [Bn.alloc(512) for _ in range(2)]
        cosb = Bn.alloc(4096); sinb = Bn.alloc(4096)
        accn = Cn.alloc(4096); accd = Cn.alloc(4096)
        stg = [Cn.alloc(1024)]
        stg_cap[0] = 1024
        tm1 = Cn.alloc(512); tm2 = Cn.alloc(512)
        scale = 1.0 / float(np.sqrt(128.0))
        for J in range(8):
            cols = slice(J * 512, (J + 1) * 512)
            P.dma(tm1[0:32, :], c_cos[:, cols], writes=["tm1"])
            P.op("dve", CALL("tensor_copy", out=cosb[0:32, cols], in_=tm1[0:32, :]), reads=["tm1"], writes=["ropetab"])
            P.dma(tm2[0:32, :], c_sin[:, cols], writes=["tm2"])
            P.op("dve", CALL("tensor_copy", out=sinb[0:32, cols], in_=tm2[0:32, :]), reads=["tm2"], writes=["ropetab"])
        gh_list = [(h, g) for h in range(4) for g in (2, 1, 0)]

        def load_gh(idx):
            h_, g_ = gh_list[idx]
            for qi in range(3):
                c0 = ATT_BASE + ((qi * 3 + g_) * 4 + h_) * 128
                load_w(Wqkv2[idx % 2][qi], wsrc[:, :, c0:c0 + 128], 8, 128, stg, ("wqkv", idx % 2), ("dve",))

        load_gh(0)
        for gi, (h, g) in enumerate(gh_list):
            if True:
                d = DIL[g]
                Wqkv = Wqkv2[gi % 2]
                wk = ("wqkv", gi % 2)

                def proj(qi, J):
                    pb = bk[J % 2]; pk = ("pb", J % 2)
                    for kc in range(8):
                        P.op("pe", CALL("matmul", pb[:, :], lhsT=Wqkv[qi][:, kc, :], rhs=hT[:, kc, J * 512:(J + 1) * 512],
                                        start=(kc == 0), stop=(kc == 7)), reads=[wk], writes=[pk])
                    P.op("act", CALL("activation", out=qkT[qi][:, J * 512:(J + 1) * 512], in_=pb[:, :], func=AF.Copy),
                         reads=[pk], writes=[("qk", qi, J)])

                def rope(qi, J):
                    dst = qkT[qi]
                    cols = slice(J * 512, (J + 1) * 512)
                    P.op("pe", CALL("matmul", bk[2][0:32, :], lhsT=pm[:, :], rhs=dst[0:32, cols], start=True, stop=True),
                         reads=[("qk", qi, J)], writes=["psw"])
                    P.op("dve", CALL("tensor_tensor", out=tm1[0:32, :], in0=bk[2][0:32, :], in1=sinb[0:32, cols], op=ALU.mult),
                         reads=["psw", "ropetab"], writes=["tm1"])
                    P.op("dve", CALL("tensor_tensor", out=tm2[0:32, :], in0=dst[0:32, cols], in1=cosb[0:32, cols], op=ALU.mult),
                         reads=[("qk", qi, J), "ropetab"], writes=["tm2"])
                    P.op("dve", CALL("tensor_tensor", out=dst[0:32, cols], in0=tm1[0:32, :], in1=tm2[0:32, :], op=ALU.add),
                         reads=["tm1", "tm2"], writes=[("qk", qi, J)])

                seq = [(qi, J) for qi in range(2) for J in range(8)]
                proj(*seq[0])
                for k_ in range(len(seq)):
                    if k_ + 1 < len(seq):
                        proj(*seq[k_ + 1])
                    rope(*seq[k_])
                for bg in range(8):
                    pb = bk[3]; pk = "pv"
                    for bi in range(4):
                        b = bg * 4 + bi
                        n_, r_ = divmod(b, d)
                        base = 128 * n_ * d + r_
                        tsl = slice(base, base + 127 * d + 1, d)
                        for kc in range(8):
                            P.op("pe", CALL("matmul", pb[:, bi * 128:(bi + 1) * 128], lhsT=hT[:, kc, tsl], rhs=Wqkv[2][:, kc, :],
                                            start=(kc == 0), stop=(kc == 7)), reads=[wk], writes=[pk])
                    P.op("act", CALL("activation", out=Vb[:, bg * 4:(bg + 1) * 4, :].rearrange("p c t -> p (c t)"), in_=bk[3][:, :], func=AF.Copy),
                         reads=[pk], writes=[("Vb", bg)])
                if gi + 1 < len(gh_list):
                    load_gh(gi + 1)
                qk_all = [("qk", qi, J) for qi in range(2) for J in range(8)]
                vb_all = [("Vb", bg) for bg in range(8)]

                def blk_info(b):
                    n_, r_ = divmod(b, d)
                    base = 128 * n_ * d + r_
                    tsl = slice(base, base + 127 * d + 1, d)
                    psl = None
                    if n_ > 0:
                        bp = 128 * (n_ - 1) * d + r_
                        psl = slice(bp, bp + 127 * d + 1, d)
                    return n_, r_, tsl, psl

                def scores(b):
                    n_, r_, tsl, psl = blk_info(b)
                    pp = b % 3
                    bSc, kSc = ((bk[4], ("bSc", 0)), (bk[5], ("bSc", 1)), (bk[2], "psw"))[pp]
                    W_ = 256 if n_ > 0 else 128
                    P.op("pe", CALL("matmul", bSc[:, 0:128], lhsT=qkT[1][:, tsl], rhs=qkT[0][:, tsl], start=True, stop=False),
                         reads=qk_all, writes=[kSc])
                    P.op("pe", CALL("matmul", bSc[:, 0:128], lhsT=ident[:], rhs=negband[:, 0:128], start=False, stop=True),
                         reads=[], writes=[kSc])
                    if n_ > 0:
                        P.op("pe", CALL("matmul", bSc[:, 128:256], lhsT=qkT[1][:, psl], rhs=qkT[0][:, tsl], start=True, stop=False),
                             reads=qk_all, writes=[kSc])
                        P.op("pe", CALL("matmul", bSc[:, 128:256], lhsT=ident[:], rhs=negband[:, 128:256], start=False, stop=True),
                             reads=[], writes=[kSc])
                    P.op("act", CALL("activation", out=pT[pp][:, 0:W_], in_=bSc[:, 0:W_], func=AF.Exp, scale=scale),
                         reads=[kSc], writes=[("pT", pp)])

                def pv(b):
                    n_, r_, tsl, psl = blk_info(b)
                    pp = b % 3
                    pn = b % 2
                    b_prev = (n_ - 1) * d + r_
                    for which in ("n", "d"):
                        if which == "n":
                            bN, nk = ((bk[6], "bNn"), (bk[0], ("pb", 0)))[pn]
                            lhs_c = Vb[:, b, :]
                            lhs_p = Vb[:, b_prev, :] if n_ > 0 else None
                            acc = accn
                        else:
                            bN, nk = ((bk[3], "pv"), (bk[1], ("pb", 1)))[pn]
                            lhs_c = onesb[:]; lhs_p = onesb[:]
                            acc = accd
                        P.op("pe", CALL("matmul", bN[:, 0:128], lhsT=lhs_c, rhs=pT[pp][:, 0:128], start=True, stop=(n_ == 0)),
                             reads=[("pT", pp)] + vb_all, writes=[nk])
                        if n_ > 0:
                            P.op("pe", CALL("matmul", bN[:, 0:128], lhsT=lhs_p, rhs=pT[pp][:, 128:256], start=False, stop=True),
                                 reads=[("pT", pp)] + vb_all, writes=[nk])
                        akey = ("acc", which)
                        if g == 2:
                            P.op("act", CALL("activation", out=acc[:, tsl], in_=bN[:, 0:128], func=AF.Copy), reads=[nk], writes=[akey])
                        else:
                            P.op("dve", CALL("tensor_tensor", out=acc[:, tsl], in0=acc[:, tsl], in1=bN[:, 0:128], op=ALU.add),
                                 reads=[nk, akey], writes=[akey])

                scores(0)
                scores(1)
                for b in range(32):
                    if b + 2 < 32:
                        scores(b + 2)
                    pv(b)
            for J in (range(8) if g == 0 else ()):
                tokJ = slice(J * 512, (J + 1) * 512)
                ob = J % 2
                P.op("act", CALL("activation", out=accd[:, tokJ], in_=accd[:, tokJ], func=AF.Ln), reads=[("acc", "d")], writes=[("acc", "d")])
                P.op("act", CALL("activation", out=accd[:, tokJ], in_=accd[:, tokJ], func=AF.Exp, scale=-1.0), reads=[("acc", "d")], writes=[("acc", "d")])
                P.op("dve", CALL("tensor_tensor", out=outt[ob], in0=accn[:, tokJ], in1=accd[:, tokJ], op=ALU.mult),
                     reads=[("acc", "d"), ("acc", "n")], writes=[("outt", ob)])
                P.dma(oatt_d[h * 128:(h + 1) * 128, tokJ], outt[ob], reads=[("outt", ob)], writes=["oatt_d"])
        P.barrier()
        if "mix" in dbg_d and li == 0:
            dump_bf(oatt_d, 512)
        if maxphase < 4:
            continue
        Bn.reset(); Cn.reset()
        Wga = v3(Bn.alloc(8192), 8); Wgb = v3(Bn.alloc(8192), 8)
        Wa = v3(Bn.alloc(4096), 4); Wb = v3(Bn.alloc(4096), 4)
        Wo = v3(Bn.alloc(8192), 8)
        ohgJ = v3(Bn.alloc(2048), 4); oattJ = v3(Bn.alloc(2048), 4)
        yT = v3(Bn.alloc(4096), 8)
        stg = [Cn.alloc(2048) for _ in range(2)]
        stg_cap[0] = 2048
        sga = Cn.alloc(512); sgb = Cn.alloc(512); ta = Cn.alloc(512); tbb = Cn.alloc(512)
        xtb = [Cn.alloc(1024) for _ in range(2)]
        load_w(Wga, wsrc[:, :, ZA_BASE:ZA_BASE + 1024], 8, 1024, stg, "wga")
        load_w(Wgb, wsrc[:, :, ZB_BASE:ZB_BASE + 1024], 8, 1024, stg, "wgb")
        load_w(Wa, w_a[l].rearrange("(kc p) n -> p kc n", p=128), 4, 1024, stg, "wa")
        load_w(Wb, w_b[l].rearrange("(kc p) n -> p kc n", p=128), 4, 1024, stg, "wb")
        load_w(Wo, w_out[l].rearrange("(kc p) n -> p kc n", p=128), 8, 1024, stg, "wo")
        ohg_v = ohg_d.rearrange("(k p) t -> p k t", p=128)
        oatt_v = oatt_d.rearrange("(k p) t -> p k t", p=128)
        for J in range(8):
            tokJ = slice(J * 512, (J + 1) * 512)
            P.dma(ohgJ, ohg_v[:, :, tokJ], writes=["ohgJ"])
            P.dma(oattJ, oatt_v[:, :, tokJ], writes=["oattJ"])
            for c in range(8):
                cs = slice(c * 128, (c + 1) * 128)
                for kc in range(8):
                    P.op("pe", CALL("matmul", bk[0][:, :], lhsT=Wga[:, kc, cs], rhs=hT[:, kc, tokJ], start=(kc == 0), stop=(kc == 7)),
                         reads=["wga", ("hTJ", J)], writes=["pga"])
                for kc in range(8):
                    P.op("pe", CALL("matmul", bk[1][:, :], lhsT=Wgb[:, kc, cs], rhs=hT[:, kc, tokJ], start=(kc == 0), stop=(kc == 7)),
                         reads=["wgb", ("hTJ", J)], writes=["pgb"])
                for kc in range(4):
                    P.op("pe", CALL("matmul", bk[2][:, :], lhsT=Wa[:, kc, cs], rhs=ohgJ[:, kc, :], start=(kc == 0), stop=(kc == 3)),
                         reads=["wa", "ohgJ"], writes=["pya"])
                for kc in range(4):
                    P.op("pe", CALL("matmul", bk[3][:, :], lhsT=Wb[:, kc, cs], rhs=oattJ[:, kc, :], start=(kc == 0), stop=(kc == 3)),
                         reads=["wb", "oattJ"], writes=["pyb"])
                P.op("act", CALL("activation", out=sga, in_=bk[0][:, :], func=AF.Sigmoid), reads=["pga"], writes=["sga"])
                P.op("act", CALL("activation", out=sgb, in_=bk[1][:, :], func=AF.Sigmoid), reads=["pgb"], writes=["sgb"])
                P.op("dve", CALL("tensor_tensor", out=ta, in0=bk[2][:, :], in1=sga, op=ALU.mult), reads=["pya", "sga"], writes=["ta"])
                P.op("dve", CALL("tensor_tensor", out=tbb, in0=bk[3][:, :], in1=sgb, op=ALU.mult), reads=["pyb", "sgb"], writes=["tb"])
                P.op("pool", CALL("tensor_tensor", out=yT[:, c, :], in0=ta, in1=tbb, op=ALU.add), reads=["ta", "tb"], writes=["yT"])
            load_w(hT[:, :, tokJ], w_up[l].rearrange("(kc p) n -> p kc n", p=128)[:, :, tokJ], 8, 512, stg, ("wup", J), ("dve", "act"), [("hTJ", J)])
            for t in range(4):
                i = 4 * J + t
                p = t % 2
                P.dma(xtb[p], xsrc[i * 128:(i + 1) * 128, :], writes=[("xt", p)])
                for half in range(2):
                    pb = bk[4 + half]; pk = ("pxo", half)
                    for c in range(8):
                        P.op("pe", CALL("matmul",
                            pb[:, :], lhsT=yT[:, c, t * 128:(t + 1) * 128], rhs=Wo[:, c, half * 512:(half + 1) * 512], start=(c == 0), stop=(c == 7)),
                            reads=["wo", "yT"], writes=[pk])
                    P.op("dve", CALL("tensor_tensor",
                        out=xtb[p][:, half * 512:(half + 1) * 512], in0=xtb[p][:, half * 512:(half + 1) * 512], in1=pb[:, :], op=ALU.add),
                        reads=[pk, ("xt", p)], writes=[("xt", p)])
                P.dma(xs[i * 128:(i + 1) * 128, :], xtb[p], reads=[("xt", p)], writes=["xs"])
                if "x1" in dbg_d and li == 0:
                    fin.append(P.dma(dbg_d["x1"][i * 128:(i + 1) * 128, :], xtb[p], reads=[("xt", p)]))
        P.barrier()

        if maxphase < 5:
            continue
        Bn.reset(); Cn.reset()
        Wup = hT
        Wdn = v3(Bn.alloc(32768), 32)
        gbc = Bn.alloc(1024)
        hs2 = Bn.alloc(1024)
        h2T = v3(Bn.alloc(2048), 8)
        rr = [Bn.alloc(512) for _ in range(2)]
        uT = v3(Bn.alloc(8192), 32)
        stg = [Cn.alloc(2048) for _ in range(3)]
        stg_cap[0] = 2048
        xtb = [Cn.alloc(1024) for _ in range(2)]
        if last:
            gfb = Cn.alloc(1024)
            P.dma(gfb, gf_d, writes=["gfb"])
        make_gbc(gbc, 8)
        wup_src = w_up[l].rearrange("(kc p) n -> p kc n", p=128)
        wdn_src = w_dn[l].rearrange("(kc p) n -> p kc n", p=128)
        for k0 in range(0, 32, 4):
            load_w(Wdn[:, k0:k0 + 4, :], wdn_src[:, k0:k0 + 4, :], 4, 1024, stg, ("wdn", k0 // 4))
        for T in range(16):
            for s_ in range(2):
                i = 2 * T + s_
                P.dma(xtb[s_], xs[i * 128:(i + 1) * 128, :], writes=[("xt", s_)])
                norm_tile(xtb[s_], ("xt", s_), h2T[:, :, s_ * 128:(s_ + 1) * 128], gbc, hs2, 0)
            for fp in range(16):
                pb = bk[fp % 3]; pk = ("pup", fp % 3)
                for j in range(2):
                    fc = 2 * fp + j
                    for kc in range(8):
                        P.op("pe", CALL("matmul",
                            pb[:, j * 256:(j + 1) * 256], lhsT=Wup[:, kc, fc * 128:(fc + 1) * 128], rhs=h2T[:, kc, :], start=(kc == 0), stop=(kc == 7)),
                            reads=[("wup", fc // 4), "hTw"], writes=[pk])
                rb = fp % 2
                P.op("act", CALL("activation", out=rr[rb], in_=pb[:, :], func=AF.Relu), reads=[pk], writes=[("rr", rb)])
                P.op("pool", CALL("tensor_tensor", out=uT[:, 2 * fp:2 * fp + 2, :].rearrange("p c t -> p (c t)"), in0=rr[rb], in1=rr[rb], op=ALU.mult),
                     reads=[("rr", rb)], writes=["uT"])
            for s_ in range(2):
                i = 2 * T + s_
                for half in range(2):
                    pb = bk[3 + half]; pk = ("pdn", half)
                    for fc in range(32):
                        P.op("pe", CALL("matmul",
                            pb[:, :], lhsT=uT[:, fc, s_ * 128:(s_ + 1) * 128], rhs=Wdn[:, fc, half * 512:(half + 1) * 512], start=(fc == 0), stop=(fc == 31)),
                            reads=[("wdn", fc // 4), "uT"], writes=[pk])
                    P.op("dve", CALL("tensor_tensor",
                        out=xtb[s_][:, half * 512:(half + 1) * 512], in0=xtb[s_][:, half * 512:(half + 1) * 512], in1=pb[:, :], op=ALU.add),
                        reads=[pk, ("xt", s_)], writes=[("xt", s_)])
                if "x2" in dbg_d and li == 0:
                    fin.append(P.dma(dbg_d["x2"][i * 128:(i + 1) * 128, :], xtb[s_], reads=[("xt", s_)]))
                if last:
                    ss = ssb[:, 2:3]; rs = ssb[:, 3:4]
                    P.op("act", CALL("activation", out=hs2, in_=xtb[s_], func=AF.Square, accum_out=ss),
                         reads=[("xt", s_)], writes=[("hs", 0), ("ss", 1)])
                    P.op("act", CALL("activation", out=rs, in_=ss, func=AF.Ln, scale=1.0 / D, bias=epsb[:, 0:1]), reads=[("ss", 1)], writes=[("rs", 1)])
                    P.op("act", CALL("activation", out=rs, in_=rs, func=AF.Exp, scale=-0.5), reads=[("rs", 1)], writes=[("rs", 1)])
                    P.op("dve", CALL("scalar_tensor_tensor", out=xtb[s_], in0=xtb[s_], scalar=rs, in1=gfb, op0=ALU.mult, op1=ALU.mult),
                         reads=[("xt", s_), ("rs", 1), "gfb"], writes=[("xt", s_)])
                    fin.append(P.dma(out_d[i * 128:(i + 1) * 128, :], xtb[s_], reads=[("xt", s_)]))
                else:
                    st_ = P.dma(xs[i * 128:(i + 1) * 128, :], xtb[s_], reads=[("xt", s_)], writes=["xs"])
                    if li == len(layers) - 1:
                        fin.append(P.dma(out_d[i * 128:(i + 1) * 128, :], xtb[s_], reads=[("xt", s_)]))
        P.barrier()

    nc = P.finalize(fin)
    return nc, P


def host_consts():
    f = np.float32
    idx = np.arange(128)
    same = (idx[:, None] // 64) == (idx[None, :] // 64)
    tri = ((idx[:, None] <= idx[None, :]) & same).astype(f)
    trev = ((idx[:, None] > idx[None, :]) & same).astype(f)
    band = np.zeros((128, 256), f)
    band[:, 0:128] = (idx[:, None] <= idx[None, :])
    band[:, 128:256] = (idx[:, None] >= idx[None, :])
    pm = np.zeros((32, 32), f)
    for dp in range(16):
        pm[dp + 16, dp] = -1.0
        pm[dp, dp + 16] = 1.0
    rm = np.zeros((128, 2), f)
    rm[0:64, 0] = 1.0
    rm[64:128, 1] = 1.0
    pos = np.arange(S, dtype=f)
    inv_freq = (np.float32(500000.0) ** (-np.arange(0, 32, 2, dtype=f) / np.float32(32))).astype(f)
    ang = (pos[:, None] * inv_freq[None, :]).astype(f)
    cos = np.cos(ang).astype(f).T
    sin = np.sin(ang).astype(f).T
    return dict(c_ident=np.eye(128, dtype=f), c_tri=tri, c_trev=trev, c_band=band, c_negband=((band - 1.0) * 30000.0).astype(f), c_pm=pm, c_rm=rm,
                c_cos=np.ascontiguousarray(np.concatenate([cos, cos], 0)),
                c_sin=np.ascontiguousarray(np.concatenate([sin, sin], 0)))


def host_params(norm1_g, w_in, hg_lower_bounds, hg_norm_g, w_branch_a, w_branch_b, w_out, norm2_g, w_up, w_down,
                final_norm_g):
    f = np.float32
    a = lambda v: np.ascontiguousarray(np.asarray(v, dtype=f))
    nl = DEPTH
    d = dict(
        w_in=a(w_in), w_a=a(w_branch_a), w_b=a(w_branch_b), w_out=a(w_out), w_up=a(w_up), w_dn=a(w_down),
        g1T=a(np.asarray(norm1_g).reshape(nl, 8, 128).transpose(0, 2, 1)),
        g2T=a(np.asarray(norm2_g).reshape(nl, 8, 128).transpose(0, 2, 1)),
        gnT=a(np.asarray(hg_norm_g).reshape(nl, 4, 128).transpose(0, 2, 1)),
        hlb=a(np.broadcast_to(np.asarray(hg_lower_bounds)[:, None, :], (nl, 128, 512))),
        gf=a(np.broadcast_to(np.asarray(final_norm_g)[None, :], (128, D))),
    )
    d.update(host_consts())
    return d


_CACHE = {}


def kernel(x, norm1_g, w_in, hg_lower_bounds, hg_norm_g, w_branch_a, w_branch_b, w_out, norm2_g, w_up, w_down,
           final_norm_g):
    x = np.asarray(x, dtype=np.float32)
    nb = x.shape[0]
    params = host_params(norm1_g, w_in, hg_lower_bounds, hg_norm_g, w_branch_a, w_branch_b, w_out, norm2_g,
                         w_up, w_down, final_norm_g)
    nc, _ = build([0, 1], True)
    in_maps = [dict(params, x=np.ascontiguousarray(x[b])) for b in range(nb)]
    res = run_bass_kernel_spmd(nc, in_maps, core_ids=list(range(nb)))
    return np.stack([np.asarray(r["out"], dtype=np.float32) for r in res.results], axis=0)
```

```python
from contextlib import ExitStack
import numpy as np
import concourse.bass as bass
import concourse.mybir as mybir
from concourse.bass_utils import run_bass_kernel_spmd

F32 = mybir.dt.float32
BF16 = mybir.dt.bfloat16
AF = mybir.ActivationFunctionType
ALU = mybir.AluOpType

S = 4096
D = 1024
NIN = 8704
DFF = 4096
EPS = 1e-6
DEPTH = 2
DIL = (1, 4, 16)
ATT_BASE = 2048
ZA_BASE = 2048 + 4608
ZB_BASE = ZA_BASE + 1024

ENGS = ("pe", "act", "dve", "pool", "sp")
STRICT_SAME = {"pe": False, "act": True, "dve": True, "pool": True, "sp": False}
DMA_RING = {"sp": 16}


class Op:
    __slots__ = ("eng", "fn", "deps", "is_dma", "slot", "need", "val", "tag")

    def __init__(self, eng, fn, deps, is_dma=False, slot=None):
        self.eng = eng
        self.fn = fn
        self.deps = deps
        self.is_dma = is_dma
        self.slot = slot
        self.need = False
        self.val = None
        self.tag = None


class Prog:
    def __init__(self):
        self.nc = bass.Bass("TRN2", target_bir_lowering=False)
        self.es = ExitStack()
        self.ops = []
        self.lastw = {}
        self.readers = {}
        self.ring_last = {q: [None] * n for q, n in DMA_RING.items()}
        self.ring_next = {q: 0 for q in DMA_RING}
        self.last_by_eng = {}
        self.dma_pending = []

    def sb(self, name, shape, dtype):
        return self.es.enter_context(self.nc.sbuf_tensor(name, list(shape), dtype))

    def ps(self, name, shape, dtype=F32):
        return self.es.enter_context(self.nc.psum_tensor(name, list(shape), dtype))

    def dram(self, name, shape, dtype, kind="Internal"):
        return self.nc.dram_tensor(name, list(shape), dtype, kind=kind)

    def _deps(self, reads, writes):
        deps = []
        for k in reads:
            w = self.lastw.get(k)
            if w is not None:
                deps.append(w)
        for k in writes:
            w = self.lastw.get(k)
            if w is not None:
                deps.append(w)
            deps.extend(self.readers.get(k, ()))
        return deps

    def _commit(self, op, reads, writes):
        for k in reads:
            lst = self.readers.setdefault(k, [])
            if not op.is_dma:
                lst[:] = [o for o in lst if o.is_dma or o.eng != op.eng]
            lst.append(op)
        for k in writes:
            self.lastw[k] = op
            self.readers[k] = []
        self.ops.append(op)
        if op.is_dma:
            self.dma_pending.append(op)
        else:
            self.last_by_eng[op.eng] = op

    def op(self, eng, fn, reads=(), writes=()):
        o = Op(eng, fn, self._deps(reads, writes))
        self._commit(o, reads, writes)
        return o

    def dma(self, out, in_, reads=(), writes=(), q="sp"):
        slot = self.ring_next[q]
        self.ring_next[q] = (slot + 1) % DMA_RING[q]
        deps = self._deps(reads, writes)
        prev = self.ring_last[q][slot]
        if prev is not None:
            deps.append(prev)
        o = Op(q, lambda e, out=out, in_=in_: e.dma_start(out=out, in_=in_).annotate(f'DMA {out.name}{list(out.shape)}@{out.offset} <- {in_.name}{list(in_.shape)}@{in_.offset}'), deps, True, slot)
        self.ring_last[q][slot] = o
        o.tag = f'{out.name}@{out.offset}<-{in_.name}@{in_.offset}'
        self._commit(o, reads, writes)
        return o

    def barrier(self):
        pend = list(self.last_by_eng.values()) + list(self.dma_pending)
        self.dma_pending = []
        for e in ENGS:
            deps = [o for o in pend if o.is_dma or o.eng != e or STRICT_SAME[e]]
            self.ops.append(Op(e, None, deps))
        self.lastw = {}
        self.readers = {}

    def finalize(self, final_waits=()):
        nc = self.nc
        for o in self.ops:
            for d in o.deps:
                if d.is_dma or d.eng != o.eng or STRICT_SAME[o.eng]:
                    d.need = True
        for o in final_waits:
            o.need = True
        csem = {e: self.es.enter_context(nc.semaphore(f"c_{e}")) for e in ENGS if e != "sp"}
        dsem = {q: [self.es.enter_context(nc.semaphore(f"d_{q}{i}")) for i in range(n)]
                for q, n in DMA_RING.items()}
        ccount = {e: 0 for e in ENGS}
        dcount = {q: [0] * n for q, n in DMA_RING.items()}
        waited = {e: {} for e in ENGS}
        streams = {e: [] for e in ENGS}
        for o in self.ops:
            st = streams[o.eng]
            w = waited[o.eng]
            need = {}
            for d in o.deps:
                if not d.is_dma and d.eng == o.eng and not STRICT_SAME[o.eng]:
                    continue
                key = ("d", d.eng, d.slot) if d.is_dma else ("c", d.eng)
                if w.get(key, 0) < d.val and need.get(key, 0) < d.val:
                    need[key] = d.val
            for key, v in need.items():
                sem = dsem[key[1]][key[2]] if key[0] == "d" else csem[key[1]]
                st.append(("w", sem, v))
                w[key] = v
            if o.fn is None:
                continue
            if o.is_dma:
                dcount[o.eng][o.slot] += 16
                o.val = dcount[o.eng][o.slot]
                st.append(("i", o.fn, dsem[o.eng][o.slot], 16, o.tag))
            elif o.need:
                ccount[o.eng] += 1
                o.val = ccount[o.eng]
                st.append(("i", o.fn, csem[o.eng], 1))
            else:
                o.val = ccount[o.eng] + 1
                st.append(("i", o.fn, None, 0))
        for o in final_waits:
            key = ("d", o.eng, o.slot) if o.is_dma else ("c", o.eng)
            sem = dsem[key[1]][key[2]] if key[0] == "d" else csem[key[1]]
            streams["sp"].append(("w", sem, o.val))
        self.streams = streams
        self.stats = dict(n_ops=len(self.ops), per_eng={e: len(s) for e, s in streams.items()}, ccount=ccount)

        def run(eng_obj, items):
            for it in items:
                if it[0] == "w":
                    eng_obj.wait_ge(it[1], it[2])
                else:
                    ins = it[1](eng_obj)
                    if it[2] is not None:
                        ins.then_inc(it[2], it[3])

        with nc.Block() as block:
            @block.tensor
            def _(e):
                run(e, streams["pe"])

            @block.scalar
            def _(e):
                run(e, streams["act"])

            @block.vector
            def _(e):
                run(e, streams["dve"])

            @block.gpsimd
            def _(e):
                run(e, streams["pool"])

            @block.sync
            def _(e):
                run(e, streams["sp"])
        self.es.close()
        return nc


class Arena:
    def __init__(self, tile, n):
        self.t = tile
        self.n = n
        self.off = 0

    def reset(self):
        self.off = 0

    def alloc(self, n):
        assert self.off + n <= self.n, (self.off, n, self.n)
        ap = self.t[:, self.off:self.off + n]
        self.off += n
        return ap


def CALL(name, *a, **k):
    return lambda e: getattr(e, name)(*a, **k)


def v3(ap, c):
    return ap.rearrange("p (c t) -> p c t", c=c)


def build(layers, final, dbg=(), maxphase=5):
    P = Prog()
    nl = DEPTH
    x_in = P.dram("x", [S, D], F32, "ExternalInput").ap()
    w_in = P.dram("w_in", [nl, D, NIN], F32, "ExternalInput").ap()
    w_a = P.dram("w_a", [nl, 512, D], F32, "ExternalInput").ap()
    w_b = P.dram("w_b", [nl, 512, D], F32, "ExternalInput").ap()
    w_out = P.dram("w_out", [nl, D, D], F32, "ExternalInput").ap()
    w_up = P.dram("w_up", [nl, D, DFF], F32, "ExternalInput").ap()
    w_dn = P.dram("w_dn", [nl, DFF, D], F32, "ExternalInput").ap()
    g1T_d = P.dram("g1T", [nl, 128, 8], F32, "ExternalInput").ap()
    g2T_d = P.dram("g2T", [nl, 128, 8], F32, "ExternalInput").ap()
    gnT_d = P.dram("gnT", [nl, 128, 4], F32, "ExternalInput").ap()
    hlb_d = P.dram("hlb", [nl, 128, 512], F32, "ExternalInput").ap()
    gf_d = P.dram("gf", [128, D], F32, "ExternalInput").ap()
    c_ident = P.dram("c_ident", [128, 128], F32, "ExternalInput").ap()
    c_tri = P.dram("c_tri", [128, 128], F32, "ExternalInput").ap()
    c_trev = P.dram("c_trev", [128, 128], F32, "ExternalInput").ap()
    c_band = P.dram("c_band", [128, 256], F32, "ExternalInput").ap()
    c_negband = P.dram("c_negband", [128, 256], F32, "ExternalInput").ap()
    c_pm = P.dram("c_pm", [32, 32], F32, "ExternalInput").ap()
    c_rm = P.dram("c_rm", [128, 2], F32, "ExternalInput").ap()
    c_cos = P.dram("c_cos", [32, S], F32, "ExternalInput").ap()
    c_sin = P.dram("c_sin", [32, S], F32, "ExternalInput").ap()
    out_d = P.dram("out", [S, D], F32, "ExternalOutput").ap()
    xs = P.dram("xs", [S, D], F32).ap()
    ohg_d = P.dram("ohg_d", [512, S], BF16).ap()
    oatt_d = P.dram("oatt_d", [512, S], BF16).ap()
    dbg_d = {k: P.dram("dbg_" + k, shp, F32, "ExternalOutput").ap() for k, shp in dbg}

    A_t = P.sb("arenaA", [128, 32768], BF16)
    B_t = P.sb("arenaB", [128, 46080], BF16)
    C_t = P.sb("arenaC", [128, 10752], F32)
    Bn = Arena(B_t, 46080)
    Cn = Arena(C_t, 10752)
    hT = v3(A_t[:, :], 8)
    ident = P.sb("ident", [128, 128], BF16)
    onesb = P.sb("onesb", [128, 128], BF16)
    trif = P.sb("trif", [128, 128], F32)
    trevf = P.sb("trevf", [128, 128], F32)
    band = P.sb("band", [128, 256], BF16)
    negband = P.sb("negband", [128, 256], BF16)
    trib = P.sb("trib", [128, 128], BF16)
    trevb = P.sb("trevb", [128, 128], BF16)
    pm = P.sb("pm", [32, 32], BF16)
    rm = P.sb("rm", [128, 2], F32)
    epsb = P.sb("epsb", [128, 1], F32)
    cf = P.sb("cf", [128, 256], F32)
    gsm = P.sb("gsm", [128, 24], F32)
    ssb = P.sb("ssb", [128, 4], F32)
    dec = P.sb("dec", [128, 16], F32)
    tp = P.ps("tp", [128, 1024], BF16)
    bk = [P.ps(f"bk{i}", [128, 512], F32) for i in range(7)]

    fin = []

    def ldconst(dst, src, rows=128, cols=128, cast=True):
        P.dma(cf[0:rows, 0:cols], src, writes=["cf"])
        P.op("dve", CALL("tensor_copy", out=dst, in_=cf[0:rows, 0:cols]), reads=["cf"], writes=["const"])

    ldconst(ident[:], c_ident)
    ldconst(band[:], c_band, 128, 256)
    ldconst(negband[:], c_negband, 128, 256)
    ldconst(trib[:], c_tri)
    ldconst(trevb[:], c_trev)
    ldconst(pm[:], c_pm, 32, 32)
    P.dma(trif[:], c_tri, writes=["const"])
    P.dma(trevf[:], c_trev, writes=["const"])
    P.dma(rm[:], c_rm, writes=["const"])
    P.op("dve", CALL("memset", onesb[:], 1.0), writes=["const"])
    P.op("dve", CALL("memset", epsb[:], EPS), writes=["const"])
    P.barrier()

    stg_i = [0]

    cast_i = [0]

    def load_w(dst3, src3, kc, n, stg, wkey="w", cast_engs=("dve", "act"), extra_w=()):
        cap = stg_cap[0]
        ncol = max(1, min(n, cap // kc))
        for c0 in range(0, n, ncol):
            c1 = min(n, c0 + ncol)
            i = stg_i[0] % len(stg)
            stg_i[0] += 1
            sv = v3(stg[i][:, 0:kc * (c1 - c0)], kc)
            P.dma(sv, src3[:, :, c0:c1], writes=[("stg", i)])
            ce = cast_engs[cast_i[0] % len(cast_engs)]
            cast_i[0] += 1
            if ce == "act":
                P.op("act", CALL("activation", out=dst3[:, :, c0:c1], in_=sv, func=AF.Copy), reads=[("stg", i)], writes=[wkey] + list(extra_w))
            else:
                P.op("dve", CALL("tensor_copy", out=dst3[:, :, c0:c1], in_=sv), reads=[("stg", i)], writes=[wkey] + list(extra_w))

    stg_cap = [2048]

    def norm_tile(xt, xkey, dst3, gbc, hs, par):
        ss = ssb[:, 2 * par:2 * par + 1]
        rs = ssb[:, 2 * par + 1:2 * par + 2]
        hk = ("hs", par)
        P.op("act", CALL("activation", out=hs, in_=xt, func=AF.Square, accum_out=ss),
             reads=[xkey], writes=[hk, ("ss", par)])
        P.op("act", CALL("activation", out=rs, in_=ss, func=AF.Ln, scale=1.0 / D, bias=epsb[:, 0:1]),
             reads=[("ss", par)], writes=[("rs", par)])
        P.op("act", CALL("activation", out=rs, in_=rs, func=AF.Exp, scale=-0.5),
             reads=[("rs", par)], writes=[("rs", par)])
        P.op("dve", CALL("tensor_scalar", out=hs, in0=xt, scalar1=rs, scalar2=None, op0=ALU.mult),
             reads=[xkey, ("rs", par)], writes=[hk])
        for c in range(8):
            P.op("pe", CALL("transpose", out=tp[:, c * 128:(c + 1) * 128], in_=hs[:, c * 128:(c + 1) * 128],
                                                  identity=ident[:]), reads=[hk], writes=["tp"])
        P.op("dve", CALL("tensor_tensor", out=dst3, in0=v3(tp[:, :], 8), in1=v3(gbc, 8), op=ALU.mult),
             reads=["tp", "gbc"], writes=["hTw"])
        return rs

    def make_gbc(gbc, col0):
        for c in range(8):
            P.op("dve", CALL("tensor_scalar", out=gbc[:, c * 128:(c + 1) * 128], in0=onesb[:],
                                                       scalar1=gsm[:, col0 + c:col0 + c + 1], scalar2=None, op0=ALU.mult),
                 reads=["gsm"], writes=["gbc"])


    def dump_bf(src_, row0):
        Bn.reset(); Cn.reset()
        dtile = Cn.alloc(4096)
        dbf = Bn.alloc(4096)
        for c in range(4):
            P.dma(dbf, src_[c * 128:(c + 1) * 128, :], writes=["dbf"])
            P.op("dve", CALL("tensor_copy", out=dtile, in_=dbf), reads=["dbf"], writes=["dt"])
            fin.append(P.dma(dbg_d["mix"][row0 + c * 128:row0 + (c + 1) * 128, :], dtile, reads=["dt"]))
        P.barrier()

    for li, l in enumerate(layers):
        xsrc = x_in if li == 0 else xs
        last = final and (li == len(layers) - 1)
        P.dma(gsm[:, 0:8], g1T_d[l], writes=["gsm"])
        P.dma(gsm[:, 8:16], g2T_d[l], writes=["gsm"])
        P.dma(gsm[:, 16:20], gnT_d[l], writes=["gsm"])

        Bn.reset(); Cn.reset()
        gbc = Bn.alloc(1024)
        hsb = [Bn.alloc(1024) for _ in range(2)]
        xtb = [Cn.alloc(1024) for _ in range(2)]
        make_gbc(gbc, 0)
        for i in range(32):
            p = i % 2
            P.dma(xtb[p], xsrc[i * 128:(i + 1) * 128, :], writes=[("xt", p)])
            norm_tile(xtb[p], ("xt", p), hT[:, :, i * 128:(i + 1) * 128], gbc, hsb[p], p)
        P.barrier()
        if "hT" in dbg_d and li == 0:
            Cn.reset()
            dtile = Cn.alloc(4096)
            for c in range(8):
                P.op("dve", CALL("tensor_copy", out=dtile, in_=hT[:, c, :]), writes=["dt"])
                fin.append(P.dma(dbg_d["hT"][c * 128:(c + 1) * 128, :], dtile, reads=["dt"]))
            P.barrier()

        if maxphase < 2:
            continue
        Bn.reset(); Cn.reset()
        Wq = v3(Bn.alloc(4096), 8); Wf = v3(Bn.alloc(4096), 8); Wi = v3(Bn.alloc(4096), 8); Wg = v3(Bn.alloc(4096), 8)
        qT = [v3(Bn.alloc(2048), 4) for _ in range(2)]
        sgT = [v3(Bn.alloc(2048), 4) for _ in range(2)]
        sq = Bn.alloc(512)
        kf = [Bn.alloc(512) for _ in range(2)]
        Vt = [Bn.alloc(512) for _ in range(3)]
        kdec = [[[Bn.alloc(128) for _ in range(2)] for _ in range(4)] for _ in range(2)]
        ktT = [Bn.alloc(128) for _ in range(2)]
        qtT = [[Bn.alloc(128) for _ in range(4)] for _ in range(2)]
        ATs = [[Bn.alloc(128) for _ in range(4)] for _ in range(2)]
        Sbf_flat = Bn.alloc(512)
        Sbf = v3(Sbf_flat, 4)
        o2 = Bn.alloc(512)
        Ghi = [Bn.alloc(512) for _ in range(2)]
        Glo = [Bn.alloc(512) for _ in range(2)]
        outt = [Bn.alloc(512) for _ in range(2)]
        stg = [Cn.alloc(2048) for _ in range(2)]
        stg_cap[0] = 2048
        Fm = Cn.alloc(512)
        Gt = [Cn.alloc(512) for _ in range(2)]
        eBT = [Cn.alloc(128) for _ in range(2)]
        enBT = [Cn.alloc(128) for _ in range(2)]
        eR = [Cn.alloc(128) for _ in range(2)]
        Sst_flat = Cn.alloc(512)
        Sst = v3(Sst_flat, 4)
        oTs = v3(Cn.alloc(2048), 4)
        rsT = Cn.alloc(512)
        if l > 0:
            lbt = Cn.alloc(512); omlb = Cn.alloc(512)
            P.dma(lbt, hlb_d[l], writes=["lbt"])
            P.dma(omlb, hlb_d[0], writes=["omlb"])
            P.op("dve", CALL("tensor_tensor", out=lbt, in0=lbt, in1=omlb, op=ALU.subtract), reads=["lbt", "omlb"], writes=["lbt"])
            P.op("act", CALL("activation", out=lbt, in_=lbt, func=AF.Sigmoid), reads=["lbt"], writes=["lbt"])
            P.op("dve", CALL("tensor_scalar", out=omlb, in0=lbt, scalar1=-1.0, scalar2=1.0, op0=ALU.mult, op1=ALU.add),
                 reads=["lbt"], writes=["omlb"])
        wsrc = w_in[l].rearrange("(kc p) n -> p kc n", p=128)
        for Wt, c0 in ((Wq, 0), (Wf, 512), (Wi, 1024), (Wg, 1536)):
            load_w(Wt, wsrc[:, :, c0:c0 + 512], 8, 512, stg)
        P.op("dve", CALL("memset", Sst_flat, 0.0), writes=[("S", h_) for h_ in range(4)])
        P.op("dve", CALL("memset", Sbf_flat, 0.0), writes=[("Sb", h_) for h_ in range(4)])
        XB = bk[3]; YB = bk[6]; FI = bk[2]
        npb = [0]

        def proj_qg(J):
            jp = J % 2
            tokJ = slice(J * 512, (J + 1) * 512)
            for h in range(4):
                hs_ = slice(h * 128, (h + 1) * 128)
                for kind, Wt in (("q", Wq), ("g", Wg)):
                    pb = bk[npb[0] % 2]; pk = ("pb", npb[0] % 2); npb[0] += 1
                    for kc in range(8):
                        P.op("pe", CALL("matmul", pb[:, :], lhsT=Wt[:, kc, hs_], rhs=hT[:, kc, tokJ], start=(kc == 0), stop=(kc == 7)),
                             reads=["w"], writes=[pk])
                    if kind == "g":
                        P.op("act", CALL("activation", out=sgT[jp][:, h, :], in_=pb[:, :], func=AF.Sigmoid), reads=[pk], writes=[("sgT", jp, h)])
                    else:
                        P.op("act", CALL("activation", out=sq, in_=pb[:, :], func=AF.Sigmoid), reads=[pk], writes=["sq"])
                        P.op("dve", CALL("tensor_tensor", out=qT[jp][:, h, :], in0=pb[:, :], in1=sq, op=ALU.mult),
                             reads=[pk, "sq"], writes=[("qT", jp, h)])

        def proj_fi(i):
            tb = i % 2; tv = i % 3
            tok = slice(i * 128, (i + 1) * 128)
            for kc in range(8):
                P.op("pe", CALL("matmul", bk[0][:, :], lhsT=hT[:, kc, tok], rhs=Wf[:, kc, :], start=(kc == 0), stop=(kc == 7)),
                     reads=["w"], writes=[("pb", 0)])
            for kc in range(8):
                P.op("pe", CALL("matmul", bk[1][:, :], lhsT=hT[:, kc, tok], rhs=Wi[:, kc, :], start=(kc == 0), stop=(kc == 7)),
                     reads=["w"], writes=[("pb", 1)])
            P.op("act", CALL("activation", out=Fm, in_=bk[0][:, :], func=AF.Sigmoid), reads=[("pb", 0)], writes=["Fm"])
            P.op("act", CALL("activation", out=Vt[tv], in_=bk[1][:, :], func=AF.Copy), reads=[("pb", 1)], writes=[("V", tv)])
            if l > 0:
                P.op("dve", CALL("tensor_tensor", out=Fm, in0=Fm, in1=omlb, op=ALU.mult), reads=["Fm", "omlb"], writes=["Fm"])
                P.op("dve", CALL("tensor_tensor", out=Fm, in0=Fm, in1=lbt, op=ALU.add), reads=["Fm", "lbt"], writes=["Fm"])
            P.op("act", CALL("activation", out=Gt[tb], in_=Fm, func=AF.Ln), reads=["Fm"], writes=[("Gf", tb)])
            P.op("dve", CALL("tensor_scalar", out=kf[tb], in0=Fm, scalar1=-1.0, scalar2=1.0, op0=ALU.mult, op1=ALU.add),
                 reads=["Fm"], writes=[("kf", tb)])
            P.op("pool", CALL("tensor_copy", out=Ghi[tb], in_=Gt[tb]), reads=[("Gf", tb)], writes=[("Ghi", tb)])
            P.op("dve", CALL("tensor_tensor", out=Glo[tb], in0=Gt[tb], in1=Ghi[tb], op=ALU.subtract),
                 reads=[("Gf", tb), ("Ghi", tb)], writes=[("G", tb)])

        def AB_A_pe(i, pair):
            tb = i % 2
            for h in pair:
                hb = h % 2; hs_ = slice(h * 128, (h + 1) * 128); bS = bk[4 + hb]; kS = ("bS", hb)
                P.op("pe", CALL("matmul", bS[:, 0:128], lhsT=Ghi[tb][:, hs_], rhs=trib[:], start=True, stop=False),
                     reads=[("G", tb), ("Ghi", tb)], writes=[kS])
                P.op("pe", CALL("matmul", bS[:, 0:128], lhsT=Glo[tb][:, hs_], rhs=trib[:], start=False, stop=True),
                     reads=[("G", tb), ("Ghi", tb)], writes=[kS])
                P.op("pe", CALL("matmul", bS[:, 128:256], lhsT=trevb[:], rhs=Ghi[tb][:, hs_], start=True, stop=False),
                     reads=[("G", tb), ("Ghi", tb)], writes=[kS])
                P.op("pe", CALL("matmul", bS[:, 128:256], lhsT=trevb[:], rhs=Glo[tb][:, hs_], start=False, stop=True),
                     reads=[("G", tb), ("Ghi", tb)], writes=[kS])
                P.op("pe", CALL("transpose", out=tp[:, hb * 128:(hb + 1) * 128], in_=kf[tb][:, hs_], identity=ident[:]),
                     reads=[("kf", tb)], writes=["tp"])

        def AB_A_act(i, pair):
            tb = i % 2
            for h in pair:
                hb = h % 2; bS = bk[4 + hb]; kS = ("bS", hb)
                P.op("act", CALL("activation", out=eBT[hb], in_=bS[:, 0:128], func=AF.Exp), reads=[kS], writes=[("eBT", hb)])
                P.op("act", CALL("activation", out=enBT[hb], in_=bS[:, 0:128], func=AF.Exp, scale=-1.0), reads=[kS], writes=[("enBT", hb)])
                P.op("act", CALL("activation", out=eR[hb], in_=bS[:, 128:256], func=AF.Exp), reads=[kS], writes=[("eR", hb)])
                di = (tb * 4 + h) * 2
                P.op("act", CALL("activation", out=dec[:, di:di + 2], in_=bS[:, 63:128:64], func=AF.Exp), reads=[kS], writes=[("dec", tb, h)])

        def AB_A_dve(i, pair):
            J, t = divmod(i, 4)
            tb = i % 2; jp = J % 2
            for h in pair:
                hb = h % 2; hs_ = slice(h * 128, (h + 1) * 128)
                for c in range(2):
                    P.op("dve", CALL("scalar_tensor_tensor", out=kdec[tb][h][c], in0=kf[tb][:, hs_], scalar=rm[:, c:c + 1], in1=eR[hb],
                                     op0=ALU.mult, op1=ALU.mult), reads=[("kf", tb), ("eR", hb)], writes=[("kdec", tb, h)])
                P.op("dve", CALL("tensor_tensor", out=ktT[hb], in0=tp[:, hb * 128:(hb + 1) * 128], in1=enBT[hb], op=ALU.mult),
                     reads=["tp", ("enBT", hb)], writes=[("ktT", hb)])
                P.op("dve", CALL("tensor_tensor", out=qtT[tb][h], in0=qT[jp][:, h, t * 128:(t + 1) * 128], in1=eBT[hb], op=ALU.mult),
                     reads=[("qT", jp, h), ("eBT", hb)], writes=[("qtT", tb, h)])

        def AB_B_pe(i, pair):
            tb = i % 2
            for h in pair:
                hb = h % 2; bS = bk[4 + hb]; kS = ("bS", hb)
                P.op("pe", CALL("matmul", bS[:, 256:384], lhsT=ktT[hb], rhs=qtT[tb][h], start=True, stop=True),
                     reads=[("ktT", hb), ("qtT", tb, h)], writes=[kS])

        def AB_B_dve(i, pair):
            tb = i % 2
            for h in pair:
                hb = h % 2; bS = bk[4 + hb]; kS = ("bS", hb)
                P.op("dve", CALL("tensor_tensor", out=ATs[tb][h], in0=bS[:, 256:384], in1=trif[:], op=ALU.mult),
                     reads=[kS], writes=[("ATs", tb, h)])

        def AB(i, pair):
            AB_A_pe(i, pair); AB_A_act(i, pair); AB_A_dve(i, pair); AB_B_pe(i, pair); AB_B_dve(i, pair)

        def xb(h, c):
            bank, key = ((bk[3], "XB0"), (FI, "pfi"))[h // 2]
            off = ((h % 2) * 2 + c) * 128
            return bank[:, off:off + 128], key

        def CF_C(i):
            tb = i % 2; tv = i % 3
            for h in range(4):
                hs_ = slice(h * 128, (h + 1) * 128)
                for c in range(2):
                    xa, xk = xb(h, c)
                    P.op("pe", CALL("matmul", xa, lhsT=kdec[tb][h][c], rhs=Vt[tv][:, hs_], start=True, stop=True),
                         reads=[("kdec", tb, h), ("V", tv)], writes=[xk])
                P.op("pe", CALL("matmul", YB[:, h * 128:h * 128 + 64], lhsT=Vt[tv][:, hs_], rhs=ATs[tb][h][:, 0:64], start=True, stop=False),
                     reads=[("V", tv), ("ATs", tb, h)], writes=["YB"])
                P.op("pe", CALL("matmul", YB[:, h * 128:h * 128 + 64], lhsT=Sbf[:, h, :], rhs=qtT[tb][h][:, 0:64], start=False, stop=True),
                     reads=[("Sb", h), ("qtT", tb, h)], writes=["YB"])

        def CF_D(i, c):
            tb = i % 2
            for h in range(4):
                di = (tb * 4 + h) * 2 + c
                xa, xk = xb(h, c)
                P.op("dve", CALL("scalar_tensor_tensor", out=Sst[:, h, :], in0=Sst[:, h, :], scalar=dec[:, di:di + 1], in1=xa,
                                 op0=ALU.mult, op1=ALU.add), reads=[xk, ("dec", tb, h), ("S", h)], writes=[("S", h)])
            for h in range(4):
                P.op("pool", CALL("tensor_copy", out=Sbf[:, h, :], in_=Sst[:, h, :]), reads=[("S", h)], writes=[("Sb", h)])

        def CF_E(i):
            tb = i % 2; tv = i % 3
            for h in range(4):
                hs_ = slice(h * 128, (h + 1) * 128)
                P.op("pe", CALL("matmul", YB[:, h * 128 + 64:h * 128 + 128], lhsT=Vt[tv][:, hs_], rhs=ATs[tb][h][:, 64:128], start=True, stop=False),
                     reads=[("V", tv), ("ATs", tb, h)], writes=["YB"])
                P.op("pe", CALL("matmul", YB[:, h * 128 + 64:h * 128 + 128], lhsT=Sbf[:, h, :], rhs=qtT[tb][h][:, 64:128], start=False, stop=True),
                     reads=[("Sb", h), ("qtT", tb, h)], writes=["YB"])

        def CF_evac(i):
            J, t = divmod(i, 4)
            P.op("act", CALL("activation", out=oTs[:, :, t * 128:(t + 1) * 128], in_=v3(YB[:, :], 4), func=AF.Copy),
                 reads=["YB"], writes=[("oTs", h_) for h_ in range(4)])

        def fin_J(J):
            jp = J % 2
            tokJ = slice(J * 512, (J + 1) * 512)
            for h in range(4):
                P.op("act", CALL("activation", out=o2, in_=oTs[:, h, :], func=AF.Square), reads=[("oTs", h)], writes=["o2"])
                P.op("pe", CALL("matmul", bk[0][:, :], lhsT=onesb[:], rhs=o2, start=True, stop=True), reads=["o2"], writes=[("pb", 0)])
                P.op("act", CALL("activation", out=rsT, in_=bk[0][:, :], func=AF.Ln, scale=1.0 / 128, bias=epsb[:, 0:1]),
                     reads=[("pb", 0)], writes=["rsT"])
                P.op("act", CALL("activation", out=rsT, in_=rsT, func=AF.Exp, scale=-0.5), reads=["rsT"], writes=["rsT"])
                P.op("dve", CALL("scalar_tensor_tensor", out=rsT, in0=oTs[:, h, :], scalar=gsm[:, 16 + h:17 + h], in1=rsT,
                                 op0=ALU.mult, op1=ALU.mult), reads=[("oTs", h), "rsT", "gsm"], writes=["rsT"])
                ob = h % 2
                P.op("dve", CALL("tensor_tensor", out=outt[ob], in0=rsT, in1=sgT[jp][:, h, :], op=ALU.mult),
                     reads=["rsT", ("sgT", jp, h)], writes=[("outt", ob)])
                P.dma(ohg_d[h * 128:(h + 1) * 128, tokJ], outt[ob], reads=[("outt", ob)], writes=["ohg_d"])

        proj_qg(0)
        proj_fi(0)
        proj_fi(1)
        AB(0, (0, 1)); AB(0, (2, 3))
        for i in range(32):
            J, t = divmod(i, 4)
            nxt = i + 1 < 32
            if nxt and (i + 1) % 4 == 0:
                proj_qg(J + 1)
            if i + 2 < 32:
                proj_fi(i + 2)
            CF_C(i)
            if nxt:
                AB_A_pe(i + 1, (0, 1))
            CF_D(i, 0)
            if nxt:
                AB_A_act(i + 1, (0, 1)); AB_A_dve(i + 1, (0, 1)); AB_B_pe(i + 1, (0, 1)); AB_B_dve(i + 1, (0, 1))
            CF_E(i)
            if nxt:
                AB_A_pe(i + 1, (2, 3))
            CF_D(i, 1)
            CF_evac(i)
            if nxt:
                AB_A_act(i + 1, (2, 3)); AB_A_dve(i + 1, (2, 3)); AB_B_pe(i + 1, (2, 3)); AB_B_dve(i + 1, (2, 3))
            if t == 3:
                fin_J(J)
        P.barrier()
        if "mix" in dbg_d and li == 0:
            dump_bf(ohg_d, 0)
        if maxphase < 3:
            continue
        Bn.reset(); Cn.reset()
        Wqkv2 = [[v3(Bn.alloc(1024), 8) for _ in range(3)] for _ in range(2)]
        qkT = [Bn.alloc(4096) for _ in range(2)]
        Vb = v3(Bn.alloc(4096), 32)
        pT = [Bn.alloc(256) for _ in range(3)]
        outt = [Bn.alloc(512) for _ in range(2)]
        cosb = Bn.alloc(4096); sinb = Bn.alloc(4096)
        accn = Cn.alloc(4096); accd = Cn.alloc(4096)
        stg = [Cn.alloc(1024)]
        stg_cap[0] = 1024
        tm1 = Cn.alloc(512); tm2 = Cn.alloc(512)
        scale = 1.0 / float(np.sqrt(128.0))
        for J in range(8):
            cols = slice(J * 512, (J + 1) * 512)
            P.dma(tm1[0:32, :], c_cos[:, cols], writes=["tm1"])
            P.op("dve", CALL("tensor_copy", out=cosb[0:32, cols], in_=tm1[0:32, :]), reads=["tm1"], writes=["ropetab"])
            P.dma(tm2[0:32, :], c_sin[:, cols], writes=["tm2"])
            P.op("dve", CALL("tensor_copy", out=sinb[0:32, cols], in_=tm2[0:32, :]), reads=["tm2"], writes=["ropetab"])
        gh_list = [(h, g) for h in range(4) for g in (2, 1, 0)]

        def load_gh(idx):
            h_, g_ = gh_list[idx]
            for qi in range(3):
                c0 = ATT_BASE + ((qi * 3 + g_) * 4 + h_) * 128
                load_w(Wqkv2[idx % 2][qi], wsrc[:, :, c0:c0 + 128], 8, 128, stg, ("wqkv", idx % 2), ("dve",))

        load_gh(0)
        for gi, (h, g) in enumerate(gh_list):
            if True:
                d = DIL[g]
                Wqkv = Wqkv2[gi % 2]
                wk = ("wqkv", gi % 2)

                def proj(qi, J):
                    pb = bk[J % 2]; pk = ("pb", J % 2)
                    for kc in range(8):
                        P.op("pe", CALL("matmul", pb[:, :], lhsT=Wqkv[qi][:, kc, :], rhs=hT[:, kc, J * 512:(J + 1) * 512],
                                        start=(kc == 0), stop=(kc == 7)), reads=[wk], writes=[pk])
                    P.op("act", CALL("activation", out=qkT[qi][:, J * 512:(J + 1) * 512], in_=pb[:, :], func=AF.Copy),
                         reads=[pk], writes=[("qk", qi, J)])

                def rope(qi, J):
                    dst = qkT[qi]
                    cols = slice(J * 512, (J + 1) * 512)
                    P.op("pe", CALL("matmul", bk[2][0:32, :], lhsT=pm[:, :], rhs=dst[0:32, cols], start=True, stop=True),
                         reads=[("qk", qi, J)], writes=["psw"])
                    P.op("dve", CALL("tensor_tensor", out=tm1[0:32, :], in0=bk[2][0:32, :], in1=sinb[0:32, cols], op=ALU.mult),
                         reads=["psw", "ropetab"], writes=["tm1"])
                    P.op("dve", CALL("tensor_tensor", out=tm2[0:32, :], in0=dst[0:32, cols], in1=cosb[0:32, cols], op=ALU.mult),
                         reads=[("qk", qi, J), "ropetab"], writes=["tm2"])
                    P.op("dve", CALL("tensor_tensor", out=dst[0:32, cols], in0=tm1[0:32, :], in1=tm2[0:32, :], op=ALU.add),
                         reads=["tm1", "tm2"], writes=[("qk", qi, J)])

                seq = [(qi, J) for qi in range(2) for J in range(8)]
                proj(*seq[0])
                for k_ in range(len(seq)):
                    if k_ + 1 < len(seq):
                        proj(*seq[k_ + 1])
                    rope(*seq[k_])
                for bg in range(8):
                    pb = bk[3]; pk = "pv"
                    for bi in range(4):
                        b = bg * 4 + bi
                        n_, r_ = divmod(b, d)
                        base = 128 * n_ * d + r_
                        tsl = slice(base, base + 127 * d + 1, d)
                        for kc in range(8):
                            P.op("pe", CALL("matmul", pb[:, bi * 128:(bi + 1) * 128], lhsT=hT[:, kc, tsl], rhs=Wqkv[2][:, kc, :],
                                            start=(kc == 0), stop=(kc == 7)), reads=[wk], writes=[pk])
                    P.op("act", CALL("activation", out=Vb[:, bg * 4:(bg + 1) * 4, :].rearrange("p c t -> p (c t)"), in_=bk[3][:, :], func=AF.Copy),
                         reads=[pk], writes=[("Vb", bg)])
                if gi + 1 < len(gh_list):
                    load_gh(gi + 1)
                qk_all = [("qk", qi, J) for qi in range(2) for J in range(8)]
                vb_all = [("Vb", bg) for bg in range(8)]

                def blk_info(b):
                    n_, r_ = divmod(b, d)
                    base = 128 * n_ * d + r_
                    tsl = slice(base, base + 127 * d + 1, d)
                    psl = None
                    if n_ > 0:
                        bp = 128 * (n_ - 1) * d + r_
                        psl = slice(bp, bp + 127 * d + 1, d)
                    return n_, r_, tsl, psl

                def scores(b):
                    n_, r_, tsl, psl = blk_info(b)
                    pp = b % 3
                    bSc, kSc = ((bk[4], ("bSc", 0)), (bk[5], ("bSc", 1)), (bk[2], "psw"))[pp]
                    W_ = 256 if n_ > 0 else 128
                    P.op("pe", CALL("matmul", bSc[:, 0:128], lhsT=qkT[1][:, tsl], rhs=qkT[0][:, tsl], start=True, stop=False),
                         reads=qk_all, writes=[kSc])
                    P.op("pe", CALL("matmul", bSc[:, 0:128], lhsT=ident[:], rhs=negband[:, 0:128], start=False, stop=True),
                         reads=[], writes=[kSc])
                    if n_ > 0:
                        P.op("pe", CALL("matmul", bSc[:, 128:256], lhsT=qkT[1][:, psl], rhs=qkT[0][:, tsl], start=True, stop=False),
                             reads=qk_all, writes=[kSc])
                        P.op("pe", CALL("matmul", bSc[:, 128:256], lhsT=ident[:], rhs=negband[:, 128:256], start=False, stop=True),
                             reads=[], writes=[kSc])
                    P.op("act", CALL("activation", out=pT[pp][:, 0:W_], in_=bSc[:, 0:W_], func=AF.Exp, scale=scale),
                         reads=[kSc], writes=[("pT", pp)])

                def pv(b):
                    n_, r_, tsl, psl = blk_info(b)
                    pp = b % 3
                    pn = b % 2
                    b_prev = (n_ - 1) * d + r_
                    for which in ("n", "d"):
                        if which == "n":
                            bN, nk = ((bk[6], "bNn"), (bk[0], ("pb", 0)))[pn]
                            lhs_c = Vb[:, b, :]
                            lhs_p = Vb[:, b_prev, :] if n_ > 0 else None
                            acc = accn
                        else:
                            bN, nk = ((bk[3], "pv"), (bk[1], ("pb", 1)))[pn]
                            lhs_c = onesb[:]; lhs_p = onesb[:]
                            acc = accd
                        P.op("pe", CALL("matmul", bN[:, 0:128], lhsT=lhs_c, rhs=pT[pp][:, 0:128], start=True, stop=(n_ == 0)),
                             reads=[("pT", pp)] + vb_all, writes=[nk])
                        if n_ > 0:
                            P.op("pe", CALL("matmul", bN[:, 0:128], lhsT=lhs_p, rhs=pT[pp][:, 128:256], start=False, stop=True),
                                 reads=[("pT", pp)] + vb_all, writes=[nk])
                        akey = ("acc", which)
                        if g == 2:
                            P.op("act", CALL("activation", out=acc[:, tsl], in_=bN[:, 0:128], func=AF.Copy), reads=[nk], writes=[akey])
                        else:
                            P.op("dve", CALL("tensor_tensor", out=acc[:, tsl], in0=acc[:, tsl], in1=bN[:, 0:128], op=ALU.add),
                                 reads=[nk, akey], writes=[akey])

                scores(0)
                scores(1)
                for b in range(32):
                    if b + 2 < 32:
                        scores(b + 2)
                    pv(b)
            for J in (range(8) if g == 0 else ()):
                tokJ = slice(J * 512, (J + 1) * 512)
                ob = J % 2
                P.op("act", CALL("activation", out=accd[:, tokJ], in_=accd[:, tokJ], func=AF.Ln), reads=[("acc", "d")], writes=[("acc", "d")])
                P.op("act", CALL("activation", out=accd[:, tokJ], in_=accd[:, tokJ], func=AF.Exp, scale=-1.0), reads=[("acc", "d")], writes=[("acc", "d")])
                P.op("dve", CALL("tensor_tensor", out=outt[ob], in0=accn[:, tokJ], in1=accd[:, tokJ], op=ALU.mult),
                     reads=[("acc", "d"), ("acc", "n")], writes=[("outt", ob)])
                P.dma(oatt_d[h * 128:(h + 1) * 128, tokJ], outt[ob], reads=[("outt", ob)], writes=["oatt_d"])
        P.barrier()
        if "mix" in dbg_d and li == 0:
            dump_bf(oatt_d, 512)
        if maxphase < 4:
            continue
        Bn.reset(); Cn.reset()
        Wga = v3(Bn.alloc(8192), 8); Wgb = v3(Bn.alloc(8192), 8)
        Wa = v3(Bn.alloc(4096), 4); Wb = v3(Bn.alloc(4096), 4)
        Wo = v3(Bn.alloc(8192), 8)
        ohgJ = v3(Bn.alloc(2048), 4); oattJ = v3(Bn.alloc(2048), 4)
        yT = v3(Bn.alloc(4096), 8)
        stg = [Cn.alloc(2048) for _ in range(2)]
        stg_cap[0] = 2048
        sga = Cn.alloc(512); sgb = Cn.alloc(512); ta = Cn.alloc(512); tbb = Cn.alloc(512)
        xtb = [Cn.alloc(1024) for _ in range(2)]
        load_w(Wga, wsrc[:, :, ZA_BASE:ZA_BASE + 1024], 8, 1024, stg, "wga")
        load_w(Wgb, wsrc[:, :, ZB_BASE:ZB_BASE + 1024], 8, 1024, stg, "wgb")
        load_w(Wa, w_a[l].rearrange("(kc p) n -> p kc n", p=128), 4, 1024, stg, "wa")
        load_w(Wb, w_b[l].rearrange("(kc p) n -> p kc n", p=128), 4, 1024, stg, "wb")
        load_w(Wo, w_out[l].rearrange("(kc p) n -> p kc n", p=128), 8, 1024, stg, "wo")
        ohg_v = ohg_d.rearrange("(k p) t -> p k t", p=128)
        oatt_v = oatt_d.rearrange("(k p) t -> p k t", p=128)
        for J in range(8):
            tokJ = slice(J * 512, (J + 1) * 512)
            P.dma(ohgJ, ohg_v[:, :, tokJ], writes=["ohgJ"])
            P.dma(oattJ, oatt_v[:, :, tokJ], writes=["oattJ"])
            for c in range(8):
                cs = slice(c * 128, (c + 1) * 128)
                for kc in range(8):
                    P.op("pe", CALL("matmul", bk[0][:, :], lhsT=Wga[:, kc, cs], rhs=hT[:, kc, tokJ], start=(kc == 0), stop=(kc == 7)),
                         reads=["wga", ("hTJ", J)], writes=["pga"])
                for kc in range(8):
                    P.op("pe", CALL("matmul", bk[1][:, :], lhsT=Wgb[:, kc, cs], rhs=hT[:, kc, tokJ], start=(kc == 0), stop=(kc == 7)),
                         reads=["wgb", ("hTJ", J)], writes=["pgb"])
                for kc in range(4):
                    P.op("pe", CALL("matmul", bk[2][:, :], lhsT=Wa[:, kc, cs], rhs=ohgJ[:, kc, :], start=(kc == 0), stop=(kc == 3)),
                         reads=["wa", "ohgJ"], writes=["pya"])
                for kc in range(4):
                    P.op("pe", CALL("matmul", bk[3][:, :], lhsT=Wb[:, kc, cs], rhs=oattJ[:, kc, :], start=(kc == 0), stop=(kc == 3)),
                         reads=["wb", "oattJ"], writes=["pyb"])
                P.op("act", CALL("activation", out=sga, in_=bk[0][:, :], func=AF.Sigmoid), reads=["pga"], writes=["sga"])
                P.op("act", CALL("activation", out=sgb, in_=bk[1][:, :], func=AF.Sigmoid), reads=["pgb"], writes=["sgb"])
                P.op("dve", CALL("tensor_tensor", out=ta, in0=bk[2][:, :], in1=sga, op=ALU.mult), reads=["pya", "sga"], writes=["ta"])
                P.op("dve", CALL("tensor_tensor", out=tbb, in0=bk[3][:, :], in1=sgb, op=ALU.mult), reads=["pyb", "sgb"], writes=["tb"])
                P.op("pool", CALL("tensor_tensor", out=yT[:, c, :], in0=ta, in1=tbb, op=ALU.add), reads=["ta", "tb"], writes=["yT"])
            load_w(hT[:, :, tokJ], w_up[l].rearrange("(kc p) n -> p kc n", p=128)[:, :, tokJ], 8, 512, stg, ("wup", J), ("dve", "act"), [("hTJ", J)])
            for t in range(4):
                i = 4 * J + t
                p = t % 2
                P.dma(xtb[p], xsrc[i * 128:(i + 1) * 128, :], writes=[("xt", p)])
                for half in range(2):
                    pb = bk[4 + half]; pk = ("pxo", half)
                    for c in range(8):
                        P.op("pe", CALL("matmul",
                            pb[:, :], lhsT=yT[:, c, t * 128:(t + 1) * 128], rhs=Wo[:, c, half * 512:(half + 1) * 512], start=(c == 0), stop=(c == 7)),
                            reads=["wo", "yT"], writes=[pk])
                    P.op("dve", CALL("tensor_tensor",
                        out=xtb[p][:, half * 512:(half + 1) * 512], in0=xtb[p][:, half * 512:(half + 1) * 512], in1=pb[:, :], op=ALU.add),
                        reads=[pk, ("xt", p)], writes=[("xt", p)])
                P.dma(xs[i * 128:(i + 1) * 128, :], xtb[p], reads=[("xt", p)], writes=["xs"])
                if "x1" in dbg_d and li == 0:
                    fin.append(P.dma(dbg_d["x1"][i * 128:(i + 1) * 128, :], xtb[p], reads=[("xt", p)]))
        P.barrier()

        if maxphase < 5:
            continue
        Bn.reset(); Cn.reset()
        Wup = hT
        Wdn = v3(Bn.alloc(32768), 32)
        gbc = Bn.alloc(1024)
        hs2 = Bn.alloc(1024)
        h2T = v3(Bn.alloc(2048), 8)
        rr = [Bn.alloc(512) for _ in range(2)]
        uT = v3(Bn.alloc(8192), 32)
        stg = [Cn.alloc(2048) for _ in range(3)]
        stg_cap[0] = 2048
        xtb = [Cn.alloc(1024) for _ in range(2)]
        if last:
            gfb = Cn.alloc(1024)
            P.dma(gfb, gf_d, writes=["gfb"])
        make_gbc(gbc, 8)
        wup_src = w_up[l].rearrange("(kc p) n -> p kc n", p=128)
        wdn_src = w_dn[l].rearrange("(kc p) n -> p kc n", p=128)
        for k0 in range(0, 32, 4):
            load_w(Wdn[:, k0:k0 + 4, :], wdn_src[:, k0:k0 + 4, :], 4, 1024, stg, ("wdn", k0 // 4))
        for T in range(16):
            for s_ in range(2):
                i = 2 * T + s_
                P.dma(xtb[s_], xs[i * 128:(i + 1) * 128, :], writes=[("xt", s_)])
                norm_tile(xtb[s_], ("xt", s_), h2T[:, :, s_ * 128:(s_ + 1) * 128], gbc, hs2, 0)
            for fp in range(16):
                pb = bk[fp % 3]; pk = ("pup", fp % 3)
                for j in range(2):
                    fc = 2 * fp + j
                    for kc in range(8):
                        P.op("pe", CALL("matmul",
                            pb[:, j * 256:(j + 1) * 256], lhsT=Wup[:, kc, fc * 128:(fc + 1) * 128], rhs=h2T[:, kc, :], start=(kc == 0), stop=(kc == 7)),
                            reads=[("wup", fc // 4), "hTw"], writes=[pk])
                rb = fp % 2
                P.op("act", CALL("activation", out=rr[rb], in_=pb[:, :], func=AF.Relu), reads=[pk], writes=[("rr", rb)])
                P.op("pool", CALL("tensor_tensor", out=uT[:, 2 * fp:2 * fp + 2, :].rearrange("p c t -> p (c t)"), in0=rr[rb], in1=rr[rb], op=ALU.mult),
                     reads=[("rr", rb)], writes=["uT"])
            for s_ in range(2):
                i = 2 * T + s_
                for half in range(2):
                    pb = bk[3 + half]; pk = ("pdn", half)
                    for fc in range(32):
                        P.op("pe", CALL("matmul",
                            pb[:, :], lhsT=uT[:, fc, s_ * 128:(s_ + 1) * 128], rhs=Wdn[:, fc, half * 512:(half + 1) * 512], start=(fc == 0), stop=(fc == 31)),
                            reads=[("wdn", fc // 4), "uT"], writes=[pk])
                    P.op("dve", CALL("tensor_tensor",
                        out=xtb[s_][:, half * 512:(half + 1) * 512], in0=xtb[s_][:, half * 512:(half + 1) * 512], in1=pb[:, :], op=ALU.add),
                        reads=[pk, ("xt", s_)], writes=[("xt", s_)])
                if "x2" in dbg_d and li == 0:
                    fin.append(P.dma(dbg_d["x2"][i * 128:(i + 1) * 128, :], xtb[s_], reads=[("xt", s_)]))
                if last:
                    ss = ssb[:, 2:3]; rs = ssb[:, 3:4]
                    P.op("act", CALL("activation", out=hs2, in_=xtb[s_], func=AF.Square, accum_out=ss),
                         reads=[("xt", s_)], writes=[("hs", 0), ("ss", 1)])
                    P.op("act", CALL("activation", out=rs, in_=ss, func=AF.Ln, scale=1.0 / D, bias=epsb[:, 0:1]), reads=[("ss", 1)], writes=[("rs", 1)])
                    P.op("act", CALL("activation", out=rs, in_=rs, func=AF.Exp, scale=-0.5), reads=[("rs", 1)], writes=[("rs", 1)])
                    P.op("dve", CALL("scalar_tensor_tensor", out=xtb[s_], in0=xtb[s_], scalar=rs, in1=gfb, op0=ALU.mult, op1=ALU.mult),
                         reads=[("xt", s_), ("rs", 1), "gfb"], writes=[("xt", s_)])
                    fin.append(P.dma(out_d[i * 128:(i + 1) * 128, :], xtb[s_], reads=[("xt", s_)]))
                else:
                    st_ = P.dma(xs[i * 128:(i + 1) * 128, :], xtb[s_], reads=[("xt", s_)], writes=["xs"])
                    if li == len(layers) - 1:
                        fin.append(P.dma(out_d[i * 128:(i + 1) * 128, :], xtb[s_], reads=[("xt", s_)]))
        P.barrier()

    nc = P.finalize(fin)
    return nc, P


def host_consts():
    f = np.float32
    idx = np.arange(128)
    same = (idx[:, None] // 64) == (idx[None, :] // 64)
    tri = ((idx[:, None] <= idx[None, :]) & same).astype(f)
    trev = ((idx[:, None] > idx[None, :]) & same).astype(f)
    band = np.zeros((128, 256), f)
    band[:, 0:128] = (idx[:, None] <= idx[None, :])
    band[:, 128:256] = (idx[:, None] >= idx[None, :])
    pm = np.zeros((32, 32), f)
    for dp in range(16):
        pm[dp + 16, dp] = -1.0
        pm[dp, dp + 16] = 1.0
    rm = np.zeros((128, 2), f)
    rm[0:64, 0] = 1.0
    rm[64:128, 1] = 1.0
    pos = np.arange(S, dtype=f)
    inv_freq = (np.float32(500000.0) ** (-np.arange(0, 32, 2, dtype=f) / np.float32(32))).astype(f)
    ang = (pos[:, None] * inv_freq[None, :]).astype(f)
    cos = np.cos(ang).astype(f).T
    sin = np.sin(ang).astype(f).T
    return dict(c_ident=np.eye(128, dtype=f), c_tri=tri, c_trev=trev, c_band=band, c_negband=((band - 1.0) * 30000.0).astype(f), c_pm=pm, c_rm=rm,
                c_cos=np.ascontiguousarray(np.concatenate([cos, cos], 0)),
                c_sin=np.ascontiguousarray(np.concatenate([sin, sin], 0)))


def host_params(norm1_g, w_in, hg_lower_bounds, hg_norm_g, w_branch_a, w_branch_b, w_out, norm2_g, w_up, w_down,
                final_norm_g):
    f = np.float32
    a = lambda v: np.ascontiguousarray(np.asarray(v, dtype=f))
    nl = DEPTH
    d = dict(
        w_in=a(w_in), w_a=a(w_branch_a), w_b=a(w_branch_b), w_out=a(w_out), w_up=a(w_up), w_dn=a(w_down),
        g1T=a(np.asarray(norm1_g).reshape(nl, 8, 128).transpose(0, 2, 1)),
        g2T=a(np.asarray(norm2_g).reshape(nl, 8, 128).transpose(0, 2, 1)),
        gnT=a(np.asarray(hg_norm_g).reshape(nl, 4, 128).transpose(0, 2, 1)),
        hlb=a(np.broadcast_to(np.asarray(hg_lower_bounds)[:, None, :], (nl, 128, 512))),
        gf=a(np.broadcast_to(np.asarray(final_norm_g)[None, :], (128, D))),
    )
    d.update(host_consts())
    return d


_CACHE = {}


def kernel(x, norm1_g, w_in, hg_lower_bounds, hg_norm_g, w_branch_a, w_branch_b, w_out, norm2_g, w_up, w_down,
           final_norm_g):
    x = np.asarray(x, dtype=np.float32)
    nb = x.shape[0]
    params = host_params(norm1_g, w_in, hg_lower_bounds, hg_norm_g, w_branch_a, w_branch_b, w_out, norm2_g,
                         w_up, w_down, final_norm_g)
    nc, _ = build([0, 1], True)
    in_maps = [dict(params, x=np.ascontiguousarray(x[b])) for b in range(nb)]
    res = run_bass_kernel_spmd(nc, in_maps, core_ids=list(range(nb)))
    return np.stack([np.asarray(r["out"], dtype=np.float32) for r in res.results], axis=0)
```
